# Optimizing a Trainium2 kernel written in Bass

```python
import jax, jax.numpy as jnp
from jax import lax
import numpy as np


D_MODEL = 1024
BATCH = 4
SEQ = 8192
DEPTH = 2

N_SUB = 3
HALF_STEP = 0.5
D_FF = 2816
CONV_WIDTH = 4
EPS = 1e-6
LRU_WIDTH = D_MODEL
LRU_HEADS = 8
LRU_BLOCK = LRU_WIDTH // LRU_HEADS
LRU_C = 8.0
SSD_WIDTH = D_MODEL
SSD_HEADDIM = 64
SSD_HEADS = SSD_WIDTH // SSD_HEADDIM
SSD_GROUPS = 2
SSD_STATE = 128
SSD_CHUNK = 128
SSD_CONV_DIM = SSD_WIDTH + 2 * SSD_GROUPS * SSD_STATE
HYB_SPLITS = (LRU_WIDTH, 2 * LRU_WIDTH, 2 * LRU_WIDTH + SSD_WIDTH, 2 * LRU_WIDTH + SSD_WIDTH + SSD_CONV_DIM)
HYB_IN = HYB_SPLITS[-1] + SSD_HEADS
HYB_OUT = LRU_WIDTH + SSD_WIDTH
MLSTM_WIDTH = 2 * D_MODEL
MLSTM_HEADS = 4
MLSTM_HEADDIM = MLSTM_WIDTH // MLSTM_HEADS
MLSTM_QKV_BLOCK = 4
MLSTM_CHUNK = 64
N_EVEN = (DEPTH + 1) // 2
N_ODD = DEPTH // 2

kernel_name = 'hybrid_rglru_ssd_mlstm_macaron_adaln'


def rms_norm(x, g):
    xf = x.astype(jnp.float32)
    y = xf * lax.rsqrt(jnp.mean(xf * xf, axis=-1, keepdims=True) + EPS)
    return (y * g.astype(jnp.float32)).astype(x.dtype)


def head_layer_norm(h, g):
    mu = jnp.mean(h, axis=-1, keepdims=True)
    hc = h - mu
    var = jnp.mean(hc * hc, axis=-1, keepdims=True)
    return hc * lax.rsqrt(var + EPS) * g.astype(jnp.float32)


def causal_conv(x, w, b):
    width, ch = w.shape
    y = lax.conv_general_dilated(x, w[:, None, :].astype(x.dtype), window_strides=(1,),
                                 padding=((width - 1, 0),), dimension_numbers=('NWC', 'WIO', 'NWC'),
                                 feature_group_count=ch)
    return y + b


def block_diag_linear(x, w):
    nb, bi, bo = w.shape
    xb = x.reshape(x.shape[:-1] + (nb, bi))
    return jnp.einsum('...ni,nio->...no', xb, w).reshape(x.shape[:-1] + (nb * bo,))


def swiglu(h, w_gate, w_up, w_down):
    return (jax.nn.silu(h @ w_gate) * (h @ w_up)) @ w_down


def rg_lru(x, w_a, b_a, w_x, b_x, lam):
    xf = x.astype(jnp.float32)
    r = jax.nn.sigmoid(block_diag_linear(xf, w_a.astype(jnp.float32)) + b_a.astype(jnp.float32))
    i = jax.nn.sigmoid(block_diag_linear(xf, w_x.astype(jnp.float32)) + b_x.astype(jnp.float32))
    log_a = -LRU_C * r * jax.nn.softplus(-lam.astype(jnp.float32))
    a = jnp.exp(log_a)
    u = jnp.sqrt(-jnp.expm1(2.0 * log_a)) * (i * xf)

    def combine(left, right):
        a_l, u_l = left
        a_r, u_r = right
        return a_l * a_r, a_r * u_l + u_r

    _, h = lax.associative_scan(combine, (a, u), axis=1)
    return h


def segsum(x):
    t = x.shape[-1]
    cs = jnp.cumsum(x, axis=-1)
    diff = cs[..., :, None] - cs[..., None, :]
    mask = jnp.tril(jnp.ones((t, t), dtype=bool))
    return jnp.where(mask, diff, -jnp.inf)


def ssd_chunked(x, dt, a, bm, cm):
    bsz, seq, nh, hp = x.shape
    ng, ns = bm.shape[-2:]
    ne = nh // ng
    nc = seq // SSD_CHUNK
    ln = SSD_CHUNK
    xc = (x * dt[..., None]).reshape(bsz, nc, ln, ng, ne, hp)
    bc = bm.reshape(bsz, nc, ln, ng, ns)
    cc = cm.reshape(bsz, nc, ln, ng, ns)
    ac = (dt * a).reshape(bsz, nc, ln, ng, ne).transpose(0, 3, 4, 1, 2)
    acs = jnp.cumsum(ac, axis=-1)
    decay_in = jnp.exp(segsum(ac))
    cb = jnp.einsum('bclgn,bcsgn->bgcls', cc, bc)
    y_diag = jnp.einsum('bgecls,bcsgep->bclgep', cb[:, :, None] * decay_in, xc)
    decay_states = jnp.exp(acs[..., -1:] - acs).transpose(0, 3, 4, 1, 2)
    states = jnp.einsum('bclgn,bclgep->bcgepn', bc, xc * decay_states[..., None])
    chunk_tot = jnp.pad(acs[..., -1], ((0, 0), (0, 0), (0, 0), (1, 0)))
    decay_chunk = jnp.exp(segsum(chunk_tot))
    states = jnp.pad(states, ((0, 0), (1, 0), (0, 0), (0, 0), (0, 0), (0, 0)))
    prev_states = jnp.einsum('bgezj,bjgepn->bzgepn', decay_chunk, states)[:, :-1]
    decay_out = jnp.exp(acs).transpose(0, 3, 4, 1, 2)
    y_off = jnp.einsum('bclgn,bcgepn->bclgep', cc, prev_states) * decay_out[..., None]
    return (y_diag + y_off).reshape(bsz, seq, nh, hp)


def mlstm_chunkwise(q, k, v, i_pre, f_pre):
    bsz, nh, seq, dh = q.shape
    nc = seq // MLSTM_CHUNK
    ln = MLSTM_CHUNK
    k = k * (dh ** -0.5)
    log_f = jax.nn.log_sigmoid(f_pre)

    def to_chunks(t):
        return jnp.moveaxis(t.reshape((bsz, nh, nc, ln) + t.shape[3:]), 2, 0)

    causal = jnp.tril(jnp.ones((ln, ln), dtype=bool))

    def step(carry, inp):
        c_st, n_st, m_st = carry
        q_c, k_c, v_c, i_c, lf_c = inp
        bcum = jnp.cumsum(lf_c, axis=-1)
        log_d = jnp.where(causal, bcum[..., :, None] - bcum[..., None, :] + i_c[..., None, :], -jnp.inf)
        g = bcum + m_st[..., None]
        m = jnp.maximum(g, jnp.max(log_d, axis=-1))
        w_inter = jnp.exp(g - m)
        s_mat = jnp.einsum('bhtd,bhsd->bhts', q_c, k_c) * jnp.exp(log_d - m[..., None])
        num = w_inter[..., None] * jnp.einsum('bhtd,bhdv->bhtv', q_c, c_st) + jnp.einsum('bhts,bhsv->bhtv', s_mat, v_c)
        den = w_inter * jnp.einsum('bhtd,bhd->bht', q_c, n_st) + jnp.sum(s_mat, axis=-1)
        h_out = num / jnp.maximum(jnp.abs(den), jnp.exp(-m))[..., None]
        b_last = bcum[..., -1]
        log_w = b_last[..., None] - bcum + i_c
        m_new = jnp.maximum(b_last + m_st, jnp.max(log_w, axis=-1))
        w_s = jnp.exp(log_w - m_new[..., None])
        decay = jnp.exp(b_last + m_st - m_new)
        c_new = decay[..., None, None] * c_st + jnp.einsum('bhs,bhsd,bhsv->bhdv', w_s, k_c, v_c)
        n_new = decay[..., None] * n_st + jnp.einsum('bhs,bhsd->bhd', w_s, k_c)
        return (c_new, n_new, m_new), h_out

    init = (jnp.zeros((bsz, nh, dh, dh), jnp.float32), jnp.zeros((bsz, nh, dh), jnp.float32),
            jnp.zeros((bsz, nh), jnp.float32))
    _, hc = lax.scan(step, init, (to_chunks(q), to_chunks(k), to_chunks(v), to_chunks(i_pre), to_chunks(log_f)))
    return jnp.moveaxis(hc, 0, 2).reshape(bsz, nh, seq, dh)


def hybrid_mixer(h, w_in, w_out, lru_conv_w, lru_conv_b, lru_wa, lru_ba, lru_wx, lru_bx, lru_lambda,
                 ssd_conv_w, ssd_conv_b, ssd_dt_bias, ssd_a_log, ssd_d, ssd_norm_g):
    bsz, seq, _ = h.shape
    f32 = jnp.float32
    gate_lru, x_lru, z_ssd, xbc, dt_raw = jnp.split(h @ w_in, HYB_SPLITS, axis=-1)
    x_lru = causal_conv(x_lru, lru_conv_w, lru_conv_b)
    y_lru = rg_lru(x_lru, lru_wa, lru_ba, lru_wx, lru_bx, lru_lambda) * jax.nn.gelu(gate_lru.astype(f32))
    xbc = jax.nn.silu(causal_conv(xbc, ssd_conv_w, ssd_conv_b)).astype(f32)
    gn = SSD_GROUPS * SSD_STATE
    xs = xbc[..., :SSD_WIDTH].reshape(bsz, seq, SSD_HEADS, SSD_HEADDIM)
    bm = xbc[..., SSD_WIDTH:SSD_WIDTH + gn].reshape(bsz, seq, SSD_GROUPS, SSD_STATE)
    cm = xbc[..., SSD_WIDTH + gn:].reshape(bsz, seq, SSD_GROUPS, SSD_STATE)
    dt = jax.nn.softplus(dt_raw.astype(f32) + ssd_dt_bias.astype(f32))
    a = -jnp.exp(ssd_a_log.astype(f32))
    y = ssd_chunked(xs, dt, a, bm, cm) + ssd_d.astype(f32)[:, None] * xs
    y = y.reshape(bsz, seq, SSD_WIDTH) * jax.nn.silu(z_ssd.astype(f32))
    y = rms_norm(y.reshape(bsz, seq, SSD_GROUPS, SSD_WIDTH // SSD_GROUPS),
                 ssd_norm_g.reshape(SSD_GROUPS, SSD_WIDTH // SSD_GROUPS)).reshape(bsz, seq, SSD_WIDTH)
    y_cat = jnp.concatenate([y_lru, y], axis=-1).astype(h.dtype)
    return y_cat @ w_out


def mlstm_block(h, w_up, conv_w, conv_b, wq, wk, wv, w_gates, b_gates, norm_g, skip, w_down):
    bsz, seq, _ = h.shape
    f32 = jnp.float32
    xm, z = jnp.split(h @ w_up, 2, axis=-1)
    xc = jax.nn.silu(causal_conv(xm, conv_w, conv_b))
    q = block_diag_linear(xc, wq)
    k = block_diag_linear(xc, wk)
    v = block_diag_linear(xm, wv)
    gates = (q @ w_gates[:MLSTM_WIDTH] + k @ w_gates[MLSTM_WIDTH:2 * MLSTM_WIDTH]
             + v @ w_gates[2 * MLSTM_WIDTH:] + b_gates).astype(f32).transpose(0, 2, 1)
    i_pre = gates[:, :MLSTM_HEADS]
    f_pre = gates[:, MLSTM_HEADS:]

    def heads(t):
        return t.astype(f32).reshape(bsz, seq, MLSTM_HEADS, MLSTM_HEADDIM).transpose(0, 2, 1, 3)

    hh = mlstm_chunkwise(heads(q), heads(k), heads(v), i_pre, f_pre)
    hh = head_layer_norm(hh.transpose(0, 2, 1, 3), norm_g.reshape(MLSTM_HEADS, MLSTM_HEADDIM))
    hh = hh.reshape(bsz, seq, MLSTM_WIDTH) + skip.astype(f32) * xc.astype(f32)
    return (hh * jax.nn.silu(z.astype(f32))).astype(h.dtype) @ w_down


def setup_inputs(seed: int = 0) -> dict:
    key = jax.random.key(seed)
    ks = iter(jax.random.split(key, 64))
    f32 = jnp.float32

    def nrm(shape, fan_in, scale=1.0):
        return jax.random.normal(next(ks), shape, f32) * (scale * fan_in ** -0.5)

    def small(shape, s=0.02):
        return jax.random.normal(next(ks), shape, f32) * s

    def gain(shape):
        return 1.0 + small(shape, 0.05)

    x = jax.random.normal(next(ks), (BATCH, SEQ, D_MODEL), f32)
    c = jax.random.normal(next(ks), (BATCH, D_MODEL), f32)
    ada_w = nrm((DEPTH, D_MODEL, N_SUB * 3 * D_MODEL), D_MODEL, 0.5)
    ada_b = small((DEPTH, N_SUB * 3 * D_MODEL))
    norm_g = gain((DEPTH, N_SUB, D_MODEL))
    ffn_w_gate = nrm((DEPTH, 2, D_MODEL, D_FF), D_MODEL)
    ffn_w_up = nrm((DEPTH, 2, D_MODEL, D_FF), D_MODEL)
    ffn_w_down = nrm((DEPTH, 2, D_FF, D_MODEL), D_FF)
    hyb_w_in = nrm((N_EVEN, D_MODEL, HYB_IN), D_MODEL)
    hyb_w_out = nrm((N_EVEN, HYB_OUT, D_MODEL), HYB_OUT)
    lru_conv_w = nrm((N_EVEN, CONV_WIDTH, LRU_WIDTH), CONV_WIDTH)
    lru_conv_b = small((N_EVEN, LRU_WIDTH))
    lru_wa = nrm((N_EVEN, LRU_HEADS, LRU_BLOCK, LRU_BLOCK), LRU_BLOCK)
    lru_ba = small((N_EVEN, LRU_WIDTH))
    lru_wx = nrm((N_EVEN, LRU_HEADS, LRU_BLOCK, LRU_BLOCK), LRU_BLOCK)
    lru_bx = small((N_EVEN, LRU_WIDTH))
    a_pow = jax.random.uniform(next(ks), (N_EVEN, LRU_WIDTH), f32, minval=0.9, maxval=0.999)
    s_a = a_pow ** (1.0 / LRU_C)
    lru_lambda = jnp.log(s_a) - jnp.log1p(-s_a)
    ssd_conv_w = nrm((N_EVEN, CONV_WIDTH, SSD_CONV_DIM), CONV_WIDTH)
    ssd_conv_b = small((N_EVEN, SSD_CONV_DIM))
    dt0 = jnp.exp(jax.random.uniform(next(ks), (N_EVEN, SSD_HEADS), f32,
                                     minval=float(np.log(1e-3)), maxval=float(np.log(1e-1))))
    ssd_dt_bias = dt0 + jnp.log(-jnp.expm1(-dt0))
    ssd_a_log = jnp.log(jax.random.uniform(next(ks), (N_EVEN, SSD_HEADS), f32, minval=1.0, maxval=16.0))
    ssd_d = gain((N_EVEN, SSD_HEADS))
    ssd_norm_g = gain((N_EVEN, SSD_WIDTH))
    mlstm_w_up = nrm((N_ODD, D_MODEL, 2 * MLSTM_WIDTH), D_MODEL)
    mlstm_conv_w = nrm((N_ODD, CONV_WIDTH, MLSTM_WIDTH), CONV_WIDTH)
    mlstm_conv_b = small((N_ODD, MLSTM_WIDTH))
    nblk = MLSTM_WIDTH // MLSTM_QKV_BLOCK
    mlstm_wq = nrm((N_ODD, nblk, MLSTM_QKV_BLOCK, MLSTM_QKV_BLOCK), MLSTM_QKV_BLOCK)
    mlstm_wk = nrm((N_ODD, nblk, MLSTM_QKV_BLOCK, MLSTM_QKV_BLOCK), MLSTM_QKV_BLOCK)
    mlstm_wv = nrm((N_ODD, nblk, MLSTM_QKV_BLOCK, MLSTM_QKV_BLOCK), MLSTM_QKV_BLOCK)
    mlstm_w_gates = nrm((N_ODD, 3 * MLSTM_WIDTH, 2 * MLSTM_HEADS), 3 * MLSTM_WIDTH, 0.5)
    f_bias = jnp.broadcast_to(jnp.linspace(3.0, 6.0, MLSTM_HEADS, dtype=f32), (N_ODD, MLSTM_HEADS))
    mlstm_b_gates = jnp.concatenate([small((N_ODD, MLSTM_HEADS), 0.1),
                                     f_bias + small((N_ODD, MLSTM_HEADS), 0.02)], axis=-1)
    mlstm_norm_g = gain((N_ODD, MLSTM_WIDTH))
    mlstm_skip = gain((N_ODD, MLSTM_WIDTH))
    mlstm_w_down = nrm((N_ODD, MLSTM_WIDTH, D_MODEL), MLSTM_WIDTH)
    final_norm_g = gain((D_MODEL,))
    return {'x': x, 'c': c, 'ada_w': ada_w, 'ada_b': ada_b, 'norm_g': norm_g,
            'ffn_w_gate': ffn_w_gate, 'ffn_w_up': ffn_w_up, 'ffn_w_down': ffn_w_down,
            'hyb_w_in': hyb_w_in, 'hyb_w_out': hyb_w_out,
            'lru_conv_w': lru_conv_w, 'lru_conv_b': lru_conv_b, 'lru_wa': lru_wa, 'lru_ba': lru_ba,
            'lru_wx': lru_wx, 'lru_bx': lru_bx, 'lru_lambda': lru_lambda,
            'ssd_conv_w': ssd_conv_w, 'ssd_conv_b': ssd_conv_b, 'ssd_dt_bias': ssd_dt_bias,
            'ssd_a_log': ssd_a_log, 'ssd_d': ssd_d, 'ssd_norm_g': ssd_norm_g,
            'mlstm_w_up': mlstm_w_up, 'mlstm_conv_w': mlstm_conv_w, 'mlstm_conv_b': mlstm_conv_b,
            'mlstm_wq': mlstm_wq, 'mlstm_wk': mlstm_wk, 'mlstm_wv': mlstm_wv,
            'mlstm_w_gates': mlstm_w_gates, 'mlstm_b_gates': mlstm_b_gates,
            'mlstm_norm_g': mlstm_norm_g, 'mlstm_skip': mlstm_skip, 'mlstm_w_down': mlstm_w_down,
            'final_norm_g': final_norm_g}


def reference(x, c, ada_w, ada_b, norm_g, ffn_w_gate, ffn_w_up, ffn_w_down, hyb_w_in, hyb_w_out,
              lru_conv_w, lru_conv_b, lru_wa, lru_ba, lru_wx, lru_bx, lru_lambda,
              ssd_conv_w, ssd_conv_b, ssd_dt_bias, ssd_a_log, ssd_d, ssd_norm_g,
              mlstm_w_up, mlstm_conv_w, mlstm_conv_b, mlstm_wq, mlstm_wk, mlstm_wv,
              mlstm_w_gates, mlstm_b_gates, mlstm_norm_g, mlstm_skip, mlstm_w_down, final_norm_g):
    bsz = x.shape[0]
    c_act = jax.nn.silu(c)
    for layer in range(DEPTH):
        mod = (c_act @ ada_w[layer] + ada_b[layer]).reshape(bsz, N_SUB, 3, D_MODEL)[:, :, :, None, :]

        def modulate(t, sub):
            return rms_norm(t, norm_g[layer, sub]) * (1.0 + mod[:, sub, 1]) + mod[:, sub, 0]

        h = modulate(x, 0)
        x = x + HALF_STEP * (1.0 + mod[:, 0, 2]) * swiglu(h, ffn_w_gate[layer, 0], ffn_w_up[layer, 0], ffn_w_down[layer, 0])
        h = modulate(x, 1)
        if layer % 2 == 0:
            e = layer // 2
            y = hybrid_mixer(h, hyb_w_in[e], hyb_w_out[e], lru_conv_w[e], lru_conv_b[e], lru_wa[e], lru_ba[e],
                             lru_wx[e], lru_bx[e], lru_lambda[e], ssd_conv_w[e], ssd_conv_b[e],
                             ssd_dt_bias[e], ssd_a_log[e], ssd_d[e], ssd_norm_g[e])
        else:
            o = layer // 2
            y = mlstm_block(h, mlstm_w_up[o], mlstm_conv_w[o], mlstm_conv_b[o], mlstm_wq[o], mlstm_wk[o],
                            mlstm_wv[o], mlstm_w_gates[o], mlstm_b_gates[o], mlstm_norm_g[o],
                            mlstm_skip[o], mlstm_w_down[o])
        x = x + (1.0 + mod[:, 1, 2]) * y.astype(x.dtype)
        h = modulate(x, 2)
        x = x + HALF_STEP * (1.0 + mod[:, 2, 2]) * swiglu(h, ffn_w_gate[layer, 1], ffn_w_up[layer, 1], ffn_w_down[layer, 1])
    return rms_norm(x, final_norm_g)
```

```python
import numpy as np
from contextlib import ExitStack
import concourse.bass as bass
import concourse.mybir as mybir
from concourse.bass_utils import run_bass_kernel_spmd

F32 = mybir.dt.float32
BF16 = mybir.dt.bfloat16
AF = mybir.ActivationFunctionType
ALU = mybir.AluOpType
AX = mybir.AxisListType


class Trk:
    __slots__ = ("w", "r", "dsem", "dcnt", "name", "dram")

    def __init__(self, name="", dram=False):
        self.dram = dram
        self.w = None
        self.r = []
        self.dsem = None
        self.dcnt = 0
        self.name = name


class V:
    __slots__ = ("ap", "trk")

    def __init__(self, ap, trk=None, name=""):
        self.ap = ap
        self.trk = trk if trk is not None else Trk(name)

    def __getitem__(self, key):
        return V(self.ap[key], self.trk)

    def re(self, ap):
        return V(ap, self.trk)


class Op:
    __slots__ = ("eng", "fn", "deps", "needed", "ticket", "dma", "trk")

    def __init__(self, eng, fn, dma=False, trk=None):
        self.eng = eng
        self.fn = fn
        self.deps = []
        self.needed = False
        self.ticket = None
        self.dma = dma
        self.trk = trk


class Rec:
    def __init__(self, nc, stack):
        self.nc = nc
        self.stack = stack
        self.ops = []
        self.E = {"pe": nc.tensor, "act": nc.scalar, "dve": nc.vector, "pool": nc.gpsimd, "sp": nc.sync}
        self.sems = {e: stack.enter_context(nc.semaphore("s_" + e)) for e in ("pe", "act", "dve", "pool")}
        self.nsem = 4
        self.out_ops = []
        self.spacer = None

    def op(self, eng, fn, w=(), r=(), dma=False):
        w = [x.trk if isinstance(x, V) else x for x in w]
        r = [x.trk if isinstance(x, V) else x for x in r]
        semtrk = None
        if dma:
            semtrk = r[0] if w[0].dram else w[0]
        w = [t for t in w if not t.dram]
        r = [t for t in r if not t.dram]
        o = Op(eng, fn, dma=dma, trk=semtrk)
        deps = {}
        for t in r:
            if t.w is not None:
                deps[id(t.w)] = (t.w, "raw")
        for t in w:
            if t.w is not None and id(t.w) not in deps:
                deps[id(t.w)] = (t.w, "waw")
            for ro in t.r:
                if id(ro) not in deps:
                    deps[id(ro)] = (ro, "war")
        for d, kind in deps.values():
            if d is o:
                continue
            if d.eng == eng and not d.dma and not dma:
                if eng == "pe":
                    continue
            o.deps.append(d)
            d.needed = True
        for t in r:
            t.r.append(o)
        for t in w:
            t.w = o
            t.r = []
        if dma:
            o.needed = True
            t = semtrk
            if t.dsem is None:
                t.dsem = self.stack.enter_context(self.nc.semaphore("d%d" % self.nsem))
                self.nsem += 1
        self.ops.append(o)
        if self.spacer is not None and eng in self.spacer and not dma:
            self.ops.append(Op(eng, self.spacer[eng]))
        return o

    def emit(self):
        cnt = {e: 0 for e in self.sems}
        seen = {e: {} for e in self.E}
        for o in self.ops:
            E = self.E[o.eng]
            sn = seen[o.eng]
            for d in o.deps:
                sem, val = d.ticket
                if sn.get(sem.num, 0) >= val:
                    continue
                E.wait_ge(sem, val)
                sn[sem.num] = val
            ins = o.fn()
            if o.dma:
                t = o.trk
                t.dcnt += 16
                ins.then_inc(t.dsem, 16)
                o.ticket = (t.dsem, t.dcnt)
            elif o.needed:
                cnt[o.eng] += 1
                ins.then_inc(self.sems[o.eng], 1)
                o.ticket = (self.sems[o.eng], cnt[o.eng])
        for o in self.out_ops:
            sem, val = o.ticket
            self.nc.sync.wait_ge(sem, val)

    def dma(self, eng, out, in_, is_out=False, **kw):
        E = self.E[eng]
        o = self.op(eng, lambda: E.dma_start(out=out.ap, in_=in_.ap, **kw), w=[out], r=[in_], dma=True)
        if is_out:
            self.out_ops.append(o)
        return o

    def mm(self, out, lhsT, rhs, start=True, stop=True, **kw):
        nc = self.nc
        return self.op("pe", lambda: nc.tensor.matmul(out.ap, lhsT.ap, rhs.ap, start=start, stop=stop, **kw),
                       w=[out], r=[lhsT, rhs])

    def transpose(self, out, in_, ident):
        nc = self.nc
        return self.op("pe", lambda: nc.tensor.transpose(out.ap, in_.ap, ident.ap), w=[out], r=[in_, ident])

    def act(self, out, in_, func, bias=None, scale=None, accum=None, eng="act"):
        nc = self.nc
        r = [in_]
        kw = {}
        if bias is not None:
            if isinstance(bias, V):
                kw["bias"] = bias.ap
                r.append(bias)
            else:
                kw["bias"] = bias
        if scale is not None:
            if isinstance(scale, V):
                kw["scale"] = scale.ap
                r.append(scale)
            else:
                kw["scale"] = scale
        w = [out]
        if accum is not None:
            kw["accum_out"] = accum.ap
            w.append(accum)
        return self.op("act", lambda: nc.scalar.activation(out=out.ap, in_=in_.ap, func=func, **kw), w=w, r=r)

    def tt(self, eng, out, in0, in1, op):
        E = self.E[eng]
        return self.op(eng, lambda: E.tensor_tensor(out=out.ap, in0=in0.ap, in1=in1.ap, op=op), w=[out], r=[in0, in1])

    def ts(self, eng, out, in0, s1, s2=None, op0=ALU.mult, op1=None, accum=None):
        E = self.E[eng]
        r = [in0]
        a1 = s1.ap if isinstance(s1, V) else s1
        a2 = s2.ap if isinstance(s2, V) else s2
        if isinstance(s1, V):
            r.append(s1)
        if isinstance(s2, V):
            r.append(s2)
        kw = {}
        if op1 is not None:
            kw["op1"] = op1
        w = [out]
        if accum is not None:
            kw["accum_out"] = accum.ap
            w.append(accum)
        return self.op(eng, lambda: E.tensor_scalar(out=out.ap, in0=in0.ap, scalar1=a1, scalar2=a2, op0=op0, **kw),
                       w=w, r=r)

    def stt(self, eng, out, in0, scalar, in1, op0, op1):
        E = self.E[eng]
        r = [in0, in1]
        a = scalar.ap if isinstance(scalar, V) else scalar
        if isinstance(scalar, V):
            r.append(scalar)
        return self.op(eng, lambda: E.scalar_tensor_tensor(out=out.ap, in0=in0.ap, scalar=a, in1=in1.ap,
                                                            op0=op0, op1=op1), w=[out], r=r)

    def copy(self, eng, out, in_):
        E = self.E[eng]
        if eng == "act":
            return self.op(eng, lambda: E.copy(out=out.ap, in_=in_.ap), w=[out], r=[in_])
        return self.op(eng, lambda: E.tensor_copy(out=out.ap, in_=in_.ap), w=[out], r=[in_])

    def memset(self, eng, out, val):
        E = self.E[eng]
        return self.op(eng, lambda: E.memset(out.ap, val), w=[out], r=[])

    def scan(self, eng, out, d0, d1, init, op0, op1):
        E = self.E[eng]
        r = [d0, d1]
        a = init.ap if isinstance(init, V) else init
        if isinstance(init, V):
            r.append(init)
        return self.op(eng, lambda: E.tensor_tensor_scan(out=out.ap, data0=d0.ap, data1=d1.ap, initial=a,
                                                         op0=op0, op1=op1), w=[out], r=r)

    def recip(self, out, in_):
        nc = self.nc
        return self.op("dve", lambda: nc.vector.reciprocal(out=out.ap, in_=in_.ap), w=[out], r=[in_])


D = 1024
DFF = 2816
KT = D // 128
FT = DFF // 128
EPS = 1e-6
HYB_IN = 4624


def col_layout(v):
    v = np.asarray(v, dtype=np.float32).reshape(-1, 128)
    return np.ascontiguousarray(v.T)


class Ctx:
    pass


def build(S, T, n_sub=6, final=True, dbg=False):
    assert S % T == 0 and T % 512 == 0
    NT = T // 512
    nc = bass.Bass("TRN2", target_bir_lowering=False)
    g = Ctx()
    g.nc = nc

    def din(name, shape, dt=F32):
        return V(nc.dram_tensor(name, list(shape), dt, kind="ExternalInput").ap(), Trk(name, dram=True))

    xT_d = din("xT", [D, S])
    outT_d = V(nc.dram_tensor("outT", [D, S], F32, kind="ExternalOutput").ap(), Trk("outT", dram=True))
    c_d = din("c_col", [128, KT])
    ada_w_d = din("ada_w", [2, D, 9 * D])
    ada_b_d = din("ada_b_col", [128, 2 * 72])
    ng_d = din("norm_g_col", [128, 2 * 3 * KT])
    fng_d = din("final_g_col", [128, KT])
    wg_d = din("ffn_w_gate", [2, 2, D, DFF])
    wu_d = din("ffn_w_up", [2, 2, D, DFF])
    wd_d = din("ffn_w_down", [2, 2, DFF, D])
    ident_d = din("ident", [128, 128])

    with ExitStack() as st:
        k = Rec(nc, st)
        g.k = k

        def sb(name, shape, dt=F32):
            return V(st.enter_context(nc.sbuf_tensor("sb_" + name, list(shape), dt))[:], name=name)

        def sbs(name, n, shape, dt=F32):
            t = st.enter_context(nc.sbuf_tensor("sb_" + name, [128, n] + list(shape), dt))
            return [V(t[:, i], name="%s%d" % (name, i)) for i in range(n)]

        ps = [V(st.enter_context(nc.psum_tensor("ps%d" % i, [128, 512], F32))[:], name="ps%d" % i) for i in range(8)]

        ones32 = sb("ones32", [128, 128])
        k.memset("dve", ones32, 1.0)
        eps_col = sb("eps_col", [128, 1])
        k.memset("dve", eps_col, EPS)
        ident = sb("ident", [128, 128])
        k.dma("sp", ident, ident_d)

        c_col = sb("c_col", [128, KT])
        k.dma("sp", c_col, c_d)
        cact = sb("cact", [128, KT])
        k.act(cact, c_col, AF.Silu)
        ada_b = sb("ada_b", [128, 144])
        k.dma("sp", ada_b, ada_b_d)
        ng = sb("ng", [128, 48])
        k.dma("sp", ng, ng_d)
        fng = sb("fng", [128, KT])
        k.dma("sp", fng, fng_d)
        mod = sb("mod", [128, 144])
        gs = sb("gs", [128, 48])
        gm = sb("gm", [128, 48])
        wa_yT = st.enter_context(nc.sbuf_tensor("sb_wa_yT", [128, KT, 512], F32))
        wa_ring = [V(wa_yT[:, 4 * i:4 * i + 4, :].rearrange("p a (b c) -> p (a b) c", b=2), name="wa%d" % i)
                   for i in range(2)]
        for layer in range(2):
            for cg in range(36):
                wa = wa_ring[cg % 2]
                k.dma("sp", wa, ada_w_d.re(ada_w_d.ap[layer, :, cg * 256:(cg + 1) * 256]
                                           .rearrange("(kt p) n -> p kt n", p=128)))
                for mi in range(2):
                    q = cg * 2 + mi
                    for kt in range(KT):
                        k.mm(ps[7][:, q:q + 1], wa[:, kt, mi * 128:(mi + 1) * 128], cact[:, kt:kt + 1],
                             start=(kt == 0), stop=(kt == KT - 1))
            k.tt("dve", mod[:, layer * 72:(layer + 1) * 72], ps[7][:, 0:72], ada_b[:, layer * 72:(layer + 1) * 72],
                 ALU.add)
            for sub in range(3):
                b0 = layer * 72 + sub * 24
                o0 = layer * 24 + sub * 8
                k.stt("dve", gs[:, o0:o0 + 8], mod[:, b0 + 8:b0 + 16], 1.0, ng[:, o0:o0 + 8], ALU.add, ALU.mult)
                k.ts("dve", gm[:, o0:o0 + 8], mod[:, b0 + 16:b0 + 24], 1.0, 1.0 if sub == 1 else 0.5,
                     op0=ALU.add, op1=ALU.mult)

        def sh_col(layer, sub, kt):
            c = layer * 72 + sub * 24 + kt
            return mod[:, c:c + 1]

        def gs_col(layer, sub, kt):
            c = layer * 24 + sub * 8 + kt
            return gs[:, c:c + 1]

        def gm_col(layer, sub, kt):
            c = layer * 24 + sub * 8 + kt
            return gm[:, c:c + 1]

        xT = sbs("xT", KT, [T])
        h = sb("h", [128, KT, T], BF16)
        sq = sbs("sq", 2, [512])
        tmpn = sbs("tmpn", KT, [512])
        rstd = sb("rstd", [128, 512])
        hid = sbs("hid", FT, [T], BF16)
        sg_ring = sbs("sg", 2, [512])
        wgu_ring = sbs("wgu", 4, [KT, 256], BF16)
        wd_ring = sbs("wd", 2, [FT, 128], BF16)
        outt = sbs("outt", KT, [T])
        g.cnt = 0

        def rms_stats(src_tiles, tb):
            sl = slice(tb * 512, (tb + 1) * 512)
            nk = len(src_tiles)
            for kt in range(nk):
                k.act(sq[kt % 2], src_tiles[kt][:, sl], AF.Square)
                k.mm(ps[6], ones32, sq[kt % 2], start=(kt == 0), stop=(kt == nk - 1))
            k.act(rstd, ps[6], AF.Sqrt, bias=eps_col[:, 0:1], scale=1.0 / (128 * nk))
            k.recip(rstd, rstd)

        def norm_mod(layer, sub):
            for tb in range(NT):
                sl = slice(tb * 512, (tb + 1) * 512)
                rms_stats(xT, tb)
                for kt in range(KT):
                    k.stt("dve", tmpn[kt], xT[kt][:, sl], gs_col(layer, sub, kt), rstd, ALU.mult, ALU.mult)
                    k.act(h[:, kt, sl], tmpn[kt], AF.Identity, bias=sh_col(layer, sub, kt))

        def ffn(layer, j):
            sub = 0 if j == 0 else 2
            norm_mod(layer, sub)
            for cg in range(11):
                wg = wgu_ring[(g.cnt % 2) * 2]
                wu = wgu_ring[(g.cnt % 2) * 2 + 1]
                g.cnt += 1
                for wt, wdram in ((wg, wg_d), (wu, wu_d)):
                    k.dma("pool", wt, wdram.re(wdram.ap[layer, j, :, cg * 256:(cg + 1) * 256]
                                               .rearrange("(kt p) n -> p kt n", p=128)))
                for mi in range(2):
                    m = cg * 2 + mi
                    for tb in range(NT):
                        sl = slice(tb * 512, (tb + 1) * 512)
                        pg = ps[(m * NT + tb) % 2]
                        pu = ps[2 + (m * NT + tb) % 2]
                        for kt in range(KT):
                            k.mm(pg, wg[:, kt, mi * 128:(mi + 1) * 128], h[:, kt, sl],
                                 start=(kt == 0), stop=(kt == KT - 1))
                        for kt in range(KT):
                            k.mm(pu, wu[:, kt, mi * 128:(mi + 1) * 128], h[:, kt, sl],
                                 start=(kt == 0), stop=(kt == KT - 1))
                        sgt = sg_ring[(m * NT + tb) % 2]
                        k.act(sgt, pg, AF.Silu)
                        k.tt("dve", hid[m][:, sl], sgt, pu, ALU.mult)
            for cg in range(8):
                wd = wd_ring[g.cnt % 2]
                g.cnt += 1
                k.dma("pool", wd, wd_d.re(wd_d.ap[layer, j, :, cg * 128:(cg + 1) * 128]
                                          .rearrange("(kt p) n -> p kt n", p=128)))
                for mi in range(1):
                    m = cg
                    for tb in range(NT):
                        sl = slice(tb * 512, (tb + 1) * 512)
                        pd = ps[4 + (m * NT + tb) % 2]
                        for kt in range(FT):
                            k.mm(pd, wd[:, kt, mi * 128:(mi + 1) * 128], hid[kt][:, sl],
                                 start=(kt == 0), stop=(kt == FT - 1))
                        k.stt("dve", xT[m][:, sl], pd, gm_col(layer, sub, m), xT[m][:, sl], ALU.mult, ALU.add)

        def final_norm(t0):
            for tb in range(NT):
                sl = slice(tb * 512, (tb + 1) * 512)
                rms_stats(xT, tb)
                for kt in range(KT):
                    k.stt("dve", outt[kt][:, sl], xT[kt][:, sl], fng[:, kt:kt + 1], rstd, ALU.mult, ALU.mult)
            for kt in range(KT):
                k.dma("sp", outT_d.re(outT_d.ap[kt * 128:(kt + 1) * 128, t0:t0 + T]), outt[kt], is_out=True)

        g.__dict__.update({kk: vv for kk, vv in locals().items() if kk != 'g'})
        mixers = make_mixers(g) if n_sub > 1 else None

        for ci in range(S // T):
            t0 = ci * T
            for kt in range(KT):
                k.dma("sp", xT[kt], xT_d.re(xT_d.ap[kt * 128:(kt + 1) * 128, t0:t0 + T]))
            si = 0
            for layer in range(2):
                for sub in range(3):
                    if si >= n_sub:
                        break
                    if sub == 1:
                        mixers[layer](ci, t0)
                    else:
                        ffn(layer, sub // 2)
                    si += 1
            if final:
                final_norm(t0)
            else:
                for kt in range(KT):
                    k.dma("sp", outT_d.re(outT_d.ap[kt * 128:(kt + 1) * 128, t0:t0 + T]), xT[kt], is_out=True)
        g.sbuf_left = nc.sbuf_bytes_remaining
        print('[build] sbuf bytes left/partition:', g.sbuf_left, 'sems:', k.nsem, 'ops:', len(k.ops))
        k.emit()
    return nc


def make_inputs(inp, b, S):
    f = lambda a: np.ascontiguousarray(np.asarray(a, dtype=np.float32))
    m = {
        "xT": f(np.asarray(inp["x"])[b, :S].T),
        "c_col": col_layout(inp["c"][b]),
        "ada_w": f(inp["ada_w"]),
        "ada_b_col": col_layout(inp["ada_b"]),
        "norm_g_col": col_layout(inp["norm_g"]),
        "final_g_col": col_layout(inp["final_norm_g"]),
        "ffn_w_gate": f(inp["ffn_w_gate"]),
        "ffn_w_up": f(inp["ffn_w_up"]),
        "ffn_w_down": f(inp["ffn_w_down"]),
        "ident": np.eye(128, dtype=np.float32),
        "tri": np.triu(np.ones((128, 128), dtype=np.float32)),
        "Umat": np.tril(np.ones((128, 128), dtype=np.float32), -1),
        "hyb_w_in": f(inp["hyb_w_in"]),
        "hyb_w_out": f(inp["hyb_w_out"]),
        "lru_wa": f(inp["lru_wa"]),
        "lru_wx": f(inp["lru_wx"]),
        "lru_conv_w_col": col_layout(inp["lru_conv_w"]),
        "lru_conv_b_col": col_layout(inp["lru_conv_b"]),
        "lru_ba_col": col_layout(inp["lru_ba"]),
        "lru_bx_col": col_layout(inp["lru_bx"]),
        "lru_lambda_col": col_layout(inp["lru_lambda"]),
        "ssd_conv_w_col": col_layout(inp["ssd_conv_w"]),
        "ssd_conv_b_col": col_layout(inp["ssd_conv_b"]),
        "ssd_dt_bias_row": f(np.tile(np.asarray(inp["ssd_dt_bias"]).reshape(1, 16), (128, 1))),
        "ssd_a_log_row": f(np.tile(np.asarray(inp["ssd_a_log"]).reshape(1, 16), (128, 1))),
        "ssd_d_col": col_layout(np.repeat(np.asarray(inp["ssd_d"]).reshape(16), 64)),
        "ssd_norm_g_col": col_layout(inp["ssd_norm_g"]),
        "mlstm_w_up": f(inp["mlstm_w_up"]),
        "mlstm_w_down": f(inp["mlstm_w_down"]),
        "bd_q": block_diag_layout(inp["mlstm_wq"]),
        "bd_k": block_diag_layout(inp["mlstm_wk"]),
        "bd_v": block_diag_layout(inp["mlstm_wv"]),
        "mlstm_w_gates": f(inp["mlstm_w_gates"]),
        "sel4": f(np.kron(np.eye(4, dtype=np.float32), np.ones((1, 128), dtype=np.float32))),
        "maskneg": f(np.where(np.triu(np.ones((128, 128))) > 0, 0.0, -30000.0)),
        "mlstm_conv_w_col": col_layout(inp["mlstm_conv_w"]),
        "mlstm_conv_b_col": col_layout(inp["mlstm_conv_b"]),
        "mlstm_norm_g_col": col_layout(inp["mlstm_norm_g"]),
        "mlstm_skip_col": col_layout(inp["mlstm_skip"]),
        "mlstm_bi_col": pad_col(np.asarray(inp["mlstm_b_gates"]).reshape(8)[0:4]),
        "mlstm_bf_col": pad_col(np.asarray(inp["mlstm_b_gates"]).reshape(8)[4:8]),
    }
    return m


def pad_col(v):
    o = np.zeros((128, 1), dtype=np.float32)
    o[:len(v), 0] = v
    return o


def block_diag_layout(w):
    w = np.asarray(w, dtype=np.float32).reshape(16, 32, 4, 4)
    o = np.zeros((16, 32, 4, 32, 4), dtype=np.float32)
    for b in range(32):
        o[:, b, :, b, :] = w[:, b]
    return np.ascontiguousarray(o.reshape(16, 128, 128))


def make_mixers(g):
    nc, k, st = g.nc, g.k, g.st
    sb, sbs, ps, din = g.sb, g.sbs, g.ps, g.din
    T = g.T
    assert T == 512
    xT, h, hid, tmpn, outt, rstd = g.xT, g.h, g.hid, g.tmpn, g.outt, g.rstd
    ident, ones32 = g.ident, g.ones32
    wgu_ring, wd_ring = g.wgu_ring, g.wd_ring
    NQ = T // 128

    tri_d = din("tri", [128, 128])
    U_d = din("Umat", [128, 128])
    tri = sb("tri", [128, 128])
    Um = sb("Um", [128, 128])
    k.dma("sp", tri, tri_d)
    k.dma("sp", Um, U_d)
    one_col = sb("one_col", [128, 1])
    k.memset("dve", one_col, 1.0)
    ps7q = [V(ps[7].ap[:, i * 128:(i + 1) * 128], name="ps7q%d" % i) for i in range(4)]

    def small(name, ncols):
        d = din(name, [128, ncols])
        t = sb("c_" + name, [128, ncols])
        k.dma("sp", t, d)
        return t

    def bc(v, a, b):
        return v.re(v.ap.unsqueeze(2).to_broadcast([128, a, b]))

    def r3(v, a, b):
        return v.re(v.ap.rearrange("p (a b) -> p a b", a=a))

    w_in_d = din("hyb_w_in", [1, D, HYB_IN])
    w_out_d = din("hyb_w_out", [1, 2048, D])
    lwa_d = din("lru_wa", [1, 8, 128, 128])
    lwx_d = din("lru_wx", [1, 8, 128, 128])
    lcw = small("lru_conv_w_col", 32)
    lcb = small("lru_conv_b_col", 8)
    lba = small("lru_ba_col", 8)
    lbx = small("lru_bx_col", 8)
    lam = small("lru_lambda_col", 8)
    scw = small("ssd_conv_w_col", 48)
    scb = small("ssd_conv_b_col", 12)
    dtb_row = small("ssd_dt_bias_row", 16)
    alog_row = small("ssd_a_log_row", 16)
    dcol = small("ssd_d_col", 8)
    sng = small("ssd_norm_g_col", 8)
    wa = sb("lwa", [128, 8, 128], BF16)
    wx = sb("lwx", [128, 8, 128], BF16)
    k.dma("pool", wa, lwa_d.re(lwa_d.ap[0].rearrange("n i o -> i n o")))
    k.dma("pool", wx, lwx_d.re(lwx_d.ap[0].rearrange("n i o -> i n o")))
    wdt = sb("wdt", [128, KT, 16], BF16)
    k.dma("pool", wdt, w_in_d.re(w_in_d.ap[0, :, 4608:4624].rearrange("(kt p) n -> p kt n", p=128)))
    clam = sb("clam", [128, 8])
    k.act(clam, lam, AF.Exp, scale=-1.0)
    k.ts("dve", clam, clam, 1.0, None, op0=ALU.add)
    k.act(clam, clam, AF.Ln)
    k.ts("dve", clam, clam, -8.0, None, op0=ALU.mult)
    ea_row = sb("ea_row", [128, 16])
    k.act(ea_row, alog_row, AF.Exp)
    lru_carry = sb("lru_carry", [128, 8, 3])
    ssd_carry = sb("ssd_carry", [128, 12, 3])
    lru_state = sb("lru_state", [128, 8])
    STf = sb("STf", [128, 1024])
    STb = sb("STb", [128, 1024], BF16)
    for t_ in (lru_carry, ssd_carry, lru_state, STf):
        k.memset("pool", t_, 0.0)
    k.memset("pool", STb, 0.0)
    tA = sb("tA", [128, T]); tB = sb("tB", [128, T]); tC = sb("tC", [128, T]); tD = sb("tD", [128, T])
    gl = sb("gl", [128, T]); xc = sb("xc", [128, T]); xcb = sb("xcb", [128, T], BF16)
    xpre = sbs("xpre", 2, [T + 3])
    zs = tmpn
    xbc = outt + sbs("xbc", 4, [T])
    bc16 = sbs("bc16", 4, [T], BF16)
    yT = [V(g.wa_yT[:, j, :], name="yT%d" % j) for j in range(8)]
    xdt = sb("xdt", [128, 1024], BF16); xdtd = sb("xdtd", [128, 1024], BF16)
    btm = sb("btm", [128, 256], BF16)
    cbm = sb("cbm", [128, 2, 128])
    Lring = sbs("Lr", 2, [128]); Ering = sbs("Er", 2, [128]); MTring = sbs("MTr", 2, [128], BF16)
    yoff = sb("yoff", [128, 512]); ytm = sb("ytm", [128, 1024])
    dtt = sb("dtt", [128, 16]); ac = sb("ac", [128, 16]); acs = sb("acs", [128, 16])
    dout = sb("dout", [128, 16]); dst = sb("dst", [128, 16]); dtot = sb("dtot", [128, 16])
    ycat = hid

    def conv(xp, j, cw, cb, ncw):
        k.ts("dve", xc, xp[:, 0:T], cw[:, j:j + 1], cb[:, j:j + 1], op0=ALU.mult, op1=ALU.add)
        for kk in range(1, 4):
            k.stt("dve", xc, xp[:, kk:kk + T], cw[:, kk * ncw + j:kk * ncw + j + 1], xc, ALU.mult, ALU.add)

    def proj(pst, wt, c0):
        for kt in range(KT):
            k.mm(pst, wt[:, kt, c0:c0 + 128], h[:, kt, :], start=(kt == 0), stop=(kt == KT - 1))

    def load_w(dram, col0, ncol=256):
        wt = wgu_ring[g.cnt % 4]
        g.cnt += 1
        k.dma("pool", wt[:, :, 0:ncol], dram.re(dram.ap[0, :, col0:col0 + ncol].rearrange("(kt p) n -> p kt n", p=128)))
        return wt

    def hybrid(ci, t0):
        g.norm_mod(0, 1)
        for hp in range(4):
            wgt = load_w(w_in_d, hp * 256)
            wxt = load_w(w_in_d, 1024 + hp * 256)
            for jj in range(2):
                j = hp * 2 + jj
                pgate, px = ps[0], ps[1]
                proj(pgate, wgt, jj * 128)
                proj(px, wxt, jj * 128)
                k.act(tA, pgate, AF.Square)
                k.ts("dve", tA, tA, 0.044715, 1.0, op0=ALU.mult, op1=ALU.add)
                k.tt("dve", tA, tA, pgate, ALU.mult)
                k.act(tA, tA, AF.Sigmoid, scale=1.5957691216057308)
                k.tt("dve", gl, tA, pgate, ALU.mult)
                xp = xpre[j % 2]
                k.copy("pool", xp[:, 0:3], lru_carry[:, j, :])
                k.copy("act", xp[:, 3:3 + T], px)
                k.copy("pool", lru_carry[:, j, :], xp[:, T:T + 3])
                conv(xp, j, lcw, lcb, 8)
                k.copy("act", xcb, xc)
                k.mm(ps[2], wa[:, j, :], xcb)
                k.mm(ps[3], wx[:, j, :], xcb)
                k.act(tB, ps[2], AF.Sigmoid, bias=lba[:, j:j + 1])
                k.act(tC, ps[3], AF.Sigmoid, bias=lbx[:, j:j + 1])
                k.act(tB, tB, AF.Exp, scale=clam[:, j:j + 1])
                k.tt("pool", tD, tB, tB, ALU.mult)
                k.act(tD, tD, AF.Sqrt, bias=one_col[:, 0:1], scale=-1.0)
                k.tt("pool", tC, tC, xc, ALU.mult)
                k.tt("dve", tC, tC, tD, ALU.mult)
                k.scan("dve", tD, tB, tC, lru_state[:, j:j + 1], ALU.mult, ALU.add)
                k.copy("pool", lru_state[:, j:j + 1], tD[:, T - 1:T])
                k.tt("dve", ycat[j], tD, gl, ALU.mult)
        for hp in range(4):
            wzt = load_w(w_in_d, 2048 + hp * 256)
            for jj in range(2):
                j = hp * 2 + jj
                proj(ps[j % 2], wzt, jj * 128)
                k.act(zs[j], ps[j % 2], AF.Silu)
        for hp in range(6):
            wbt = load_w(w_in_d, 3072 + hp * 256)
            for jj in range(2):
                j = hp * 2 + jj
                proj(ps[j % 2], wbt, jj * 128)
                xp = xpre[j % 2]
                k.copy("pool", xp[:, 0:3], ssd_carry[:, j, :])
                k.copy("act", xp[:, 3:3 + T], ps[j % 2])
                k.copy("pool", ssd_carry[:, j, :], xp[:, T:T + 3])
                conv(xp, j, scw, scb, 12)
                k.act(xbc[j], xc, AF.Silu)
                if j >= 8:
                    k.copy("pool", bc16[j - 8], xbc[j])
        for q in range(NQ):
            qs = slice(q * 128, (q + 1) * 128)
            for kt in range(KT):
                k.mm(ps[0][:, 256:272], h[:, kt, qs], wdt[:, kt, :], start=(kt == 0), stop=(kt == KT - 1))
            k.tt("dve", dtt, ps[0][:, 256:272], dtb_row, ALU.add)
            k.act(dtt, dtt, AF.Exp)
            k.ts("dve", dtt, dtt, 1.0, None, op0=ALU.add)
            k.act(dtt, dtt, AF.Ln)
            k.stt("dve", ac, dtt, -1.0, ea_row, ALU.mult, ALU.mult)
            k.mm(ps[0][:, 272:288], tri, ac)
            k.mm(ps[0][:, 288:304], ones32, ac)
            k.copy("act", acs, ps[0][:, 272:288])
            k.act(dout, acs, AF.Exp)
            k.tt("dve", dst, ps[0][:, 288:304], acs, ALU.subtract)
            k.act(dst, dst, AF.Exp)
            k.act(dtot, ps[0][:, 288:304], AF.Exp)
            for j in range(8):
                k.transpose(ps[2 + j // 4][:, (j % 4) * 128:(j % 4 + 1) * 128], xbc[j][:, qs], ident)
            for half in range(2):
                hs = slice(half * 512, (half + 1) * 512)
                k.tt("dve", r3(xdt[:, hs], 8, 64), r3(ps[2 + half], 8, 64), bc(dtt[:, half * 8:half * 8 + 8], 8, 64),
                     ALU.mult)
                k.tt("pool", r3(xdtd[:, hs], 8, 64), r3(xdt[:, hs], 8, 64), bc(dst[:, half * 8:half * 8 + 8], 8, 64),
                     ALU.mult)
            for gi in range(2):
                k.transpose(ps[0][:, gi * 128:(gi + 1) * 128], xbc[8 + gi][:, qs], ident)
            k.copy("act", btm, ps[0][:, 0:256])
            for gi in range(2):
                k.mm(ps[1][:, gi * 128:(gi + 1) * 128], bc16[gi][:, qs], bc16[2 + gi][:, qs])
            k.tt("dve", cbm, r3(ps[1][:, 0:256], 2, 128),
                 tri.re(tri.ap.unsqueeze(1).to_broadcast([128, 2, 128])), ALU.mult)
            for gi in range(2):
                gs_ = slice(gi * 512, (gi + 1) * 512)
                for e in range(8):
                    hh = gi * 8 + e
                    Lh = Lring[hh % 2]
                    k.ts("dve", Lh, Um, ac[:, hh:hh + 1], None, op0=ALU.mult)
                    pq = ps7q[hh % 4]
                    k.mm(pq, Lh, tri)
                    Eh = Ering[hh % 2]
                    k.act(Eh, pq, AF.Exp)
                    MT = MTring[hh % 2]
                    k.tt("dve", MT, Eh, cbm[:, gi, :], ALU.mult)
                    k.mm(ps[4 + gi][:, e * 64:(e + 1) * 64], MT, xdt[:, hh * 64:(hh + 1) * 64])
                k.mm(ps[6], bc16[2 + gi][:, qs], STb[:, gs_])
                k.tt("dve", r3(yoff, 8, 64), r3(ps[6], 8, 64), bc(dout[:, gi * 8:gi * 8 + 8], 8, 64), ALU.mult)
                k.tt("dve", ytm[:, gs_], ps[4 + gi], yoff, ALU.add)
                k.mm(ps[6], btm[:, gi * 128:(gi + 1) * 128], xdtd[:, gs_])
                k.tt("pool", r3(STf[:, gs_], 8, 64), r3(STf[:, gs_], 8, 64), bc(dtot[:, gi * 8:gi * 8 + 8], 8, 64),
                     ALU.mult)
                k.tt("dve", STf[:, gs_], STf[:, gs_], ps[6], ALU.add)
                k.copy("act", STb[:, gs_], STf[:, gs_])
            for j in range(8):
                k.transpose(ps[2 + j // 4][:, (j % 4) * 128:(j % 4 + 1) * 128], ytm[:, j * 128:(j + 1) * 128], ident)
            for j in range(8):
                k.stt("dve", yT[j][:, qs], xbc[j][:, qs], dcol[:, j:j + 1],
                      ps[2 + j // 4][:, (j % 4) * 128:(j % 4 + 1) * 128], ALU.mult, ALU.add)
        for j in range(8):
            k.tt("pool", yT[j], yT[j], zs[j], ALU.mult)
        for gi in range(2):
            g.rms_stats(yT[gi * 4:(gi + 1) * 4], 0)
            for jj in range(4):
                j = gi * 4 + jj
                k.stt("dve", ycat[8 + j], yT[j], sng[:, j:j + 1], rstd, ALU.mult, ALU.mult)
        for m in range(8):
            wo = wd_ring[g.cnt % 2]
            g.cnt += 1
            k.dma("pool", wo[:, 0:16, :], w_out_d.re(w_out_d.ap[0, :, m * 128:(m + 1) * 128]
                                                     .rearrange("(kt p) n -> p kt n", p=128)))
            pd = ps[4 + m % 2]
            for kt in range(16):
                k.mm(pd, wo[:, kt, :], ycat[kt], start=(kt == 0), stop=(kt == 15))
            k.stt("dve", xT[m], pd, g.gm_col(0, 1, m), xT[m], ALU.mult, ALU.add)

    w_up_d = din("mlstm_w_up", [1, D, 4096])
    w_dn_d = din("mlstm_w_down", [1, 2048, D])
    bdq_d = din("bd_q", [16, 128, 128])
    bdk_d = din("bd_k", [16, 128, 128])
    bdv_d = din("bd_v", [16, 128, 128])
    wgt_d = din("mlstm_w_gates", [1, 6144, 8])
    sel_d = din("sel4", [4, 512])
    mneg_d = din("maskneg", [128, 128])
    cst_d = V(nc.dram_tensor("c_state", [4, 512, 512], F32, kind="ExternalOutput").ap(), name="c_state")
    mcw = small("mlstm_conv_w_col", 64)
    mcb = small("mlstm_conv_b_col", 16)
    mng = small("mlstm_norm_g_col", 16)
    msk = small("mlstm_skip_col", 16)
    bgi = small("mlstm_bi_col", 1)
    bgf = small("mlstm_bf_col", 1)
    maskneg = sb("maskneg", [128, 128])
    k.dma("sp", maskneg, mneg_d)
    wgi = sb("wgi", [128, 48, 4], BF16)
    wgf = sb("wgf", [128, 48, 4], BF16)
    k.dma("pool", wgi, wgt_d.re(wgt_d.ap[0, :, 0:4].rearrange("(kt p) n -> p kt n", p=128)))
    k.dma("pool", wgf, wgt_d.re(wgt_d.ap[0, :, 4:8].rearrange("(kt p) n -> p kt n", p=128)))
    ones_bf = sb("ones_bf", [128, 1], BF16)
    k.memset("dve", ones_bf, 1.0)
    ml_carry = sb("ml_carry", [128, 16, 3])
    nf = sb("nf", [128, 16])
    nb = sb("nb", [128, 16, 2], BF16)
    mu_rep = sb("mu_rep", [128, 4])
    Fc = sb("Fc", [128, 1])
    Mc = sb("Mc", [128, 1])
    for t_ in (ml_carry, nf, mu_rep, Fc, Mc):
        k.memset("pool", t_, 0.0)
    k.memset("pool", nb, 0.0)
    Cf = sb("Cf", [128, 4, 512])
    xmb_x = sb("xmb_x", [128, T], BF16)
    qkvt = sb("qkvt", [128, T], BF16)
    qTb = sb("qTb", [128, 4, 128], BF16); qwb = sb("qwb", [128, 4, 128], BF16); kTb = sb("kTb", [128, 4, 128], BF16)
    ktw = sb("ktw", [128, 512], BF16); vtm = sb("vtm", [128, 512], BF16)
    cols = sb("mlcols", [128, 16])
    xcbm = hid[0:16]
    xmb = hid[16:22] + bc16 + [xcb, xdt[:, 0:512], xdt[:, 512:1024], xdtd[:, 0:512], xdtd[:, 512:1024], xmb_x]
    fm = tmpn + xbc[0:8]
    ipre, fl, Fr, gg, Mr, emt, onesT, sel4 = [yT[i] for i in range(8)]
    BD = [wgu_ring[i].re(wgu_ring[i].ap.rearrange("p a (b c) -> p (a b) c", b=2)) for i in range(3)]
    Cb = wgu_ring[3].re(wgu_ring[3].ap.rearrange("p (a b) c -> p a (b c)", b=2))
    wz_t = wd_ring[0].re(wd_ring[0].ap[:, 0:16, :].rearrange("p (a b) c -> p a (b c)", b=2))
    tmpD, wrow = Lring[0], Lring[1]
    Dt_, smat, hs = Ering[0], MTring[0], tC
    KS = 512 ** -0.5
    spc = sb("spc", [128, 128])
    spacer = {"act": lambda: nc.scalar.copy(out=spc.ap, in_=ones32.ap),
              "dve": lambda: nc.vector.tensor_copy(out=spc.ap, in_=ones32.ap)}

    def mlstm(ci, t0):
        g.norm_mod(1, 1)
        for i, d_ in enumerate((bdq_d, bdk_d, bdv_d)):
            k.dma("pool", BD[i], d_.re(d_.ap.rearrange("n i o -> i n o")))
        k.dma("sp", sel4[0:4, :], sel_d)
        k.memset("pool", onesT[0:4, :], 1.0)
        import os as _os
        _stage = int(_os.environ.get("MLSTM_STAGE", "9"))
        if _stage <= 0:
            return
        for j in range(16):
            wt = wgu_ring[3]
            if j % 2 == 0:
                k.dma("pool", wt, w_up_d.re(w_up_d.ap[0, :, j * 128:(j + 2) * 128]
                                            .rearrange("(kt p) n -> p kt n", p=128)))
            px = ps[j % 2]
            proj(px, wt, (j % 2) * 128)
            xp = xpre[j % 2]
            k.copy("pool", xp[:, 0:3], ml_carry[:, j, :])
            k.copy("act", xp[:, 3:3 + T], px)
            k.copy("pool", ml_carry[:, j, :], xp[:, T:T + 3])
            k.copy("act", xmb[j], px)
            conv(xp, j, mcw, mcb, 16)
            k.act(xcbm[j], xc, AF.Silu)
            for wi, (src, bdm) in enumerate(((xcbm[j], BD[0]), (xcbm[j], BD[1]), (xmb[j], BD[2]))):
                if _os.environ.get("MLSTM_NOBD"):
                    continue
                k.mm(ps[2 + wi % 2], bdm[:, j, :], src)
                k.copy("act", qkvt, ps[2 + wi % 2])
                first = (j == 0 and wi == 0)
                last = (j == 15 and wi == 2)
                if _os.environ.get("MLSTM_NOGATE"):
                    continue
                k.mm(ps[7][0:4, :], wgi[:, wi * 16 + j, :], qkvt, start=first, stop=last)
                k.mm(ps[6][0:4, :], wgf[:, wi * 16 + j, :], qkvt, start=first, stop=last)
        import os as _os
        _stage = int(_os.environ.get("MLSTM_STAGE", "9"))
        if _stage <= 1:
            return
        I4 = slice(0, 4)
        k.act(ipre[I4, :], ps[7][0:4, :], AF.Identity, bias=bgi[0:4, 0:1])
        k.act(fl[I4, :], ps[6][0:4, :], AF.Identity, bias=bgf[0:4, 0:1])
        k.act(fl[I4, :], fl[I4, :], AF.Exp, scale=-1.0)
        k.ts("dve", fl[I4, :], fl[I4, :], 1.0, None, op0=ALU.add)
        k.act(fl[I4, :], fl[I4, :], AF.Ln)
        k.ts("dve", fl[I4, :], fl[I4, :], -1.0, None, op0=ALU.mult)
        k.scan("dve", Fr[I4, :], onesT[I4, :], fl[I4, :], Fc[0:4, 0:1], ALU.mult, ALU.add)
        k.tt("dve", gg[I4, :], ipre[I4, :], Fr[I4, :], ALU.subtract)
        k.scan("dve", Mr[I4, :], gg[I4, :], gg[I4, :], Mc[0:4, 0:1], ALU.max, ALU.max)
        k.copy("pool", Fc[0:4, 0:1], Fr[I4, T - 1:T])
        k.copy("pool", Mc[0:4, 0:1], Mr[I4, T - 1:T])
        k.tt("dve", emt[I4, :], Fr[I4, :], Mr[I4, :], ALU.add)
        k.act(emt[I4, :], emt[I4, :], AF.Exp, scale=-1.0)
        if _stage <= 2:
            return
        k.spacer = spacer
        for hd in range(4):
            cs_v = cst_d.re(cst_d.ap[hd].rearrange("(dt p) v -> p dt v", p=128))
            if ci == 0:
                k.memset("pool", Cf, 0.0)
            else:
                k.dma("sp", Cf, cs_v)
            k.copy("act", Cb, Cf)
            k.mm(ps[0], sel4[0:4, hd * 128:(hd + 1) * 128], Mr[I4, :])
            Mbc = tA
            k.copy("act", Mbc, ps[0])
            for q in range(NQ):
                if _stage <= 3:
                    continue
                qs = slice(q * 128, (q + 1) * 128)
                k.transpose(ps[6][:, 136:140], gg[I4, qs], ident[0:4, 0:4])
                k.transpose(ps[6][:, 140:144], emt[I4, qs], ident[0:4, 0:4])
                k.copy("act", cols[:, 0:8], ps[6][:, 136:144])
                _sub = int(_os.environ.get("MLSTM_SUB", "9"))
                if _sub <= 1:
                    continue
                mu_st = mu_rep[:, hd:hd + 1]
                mu_new = Mbc[:, (q + 1) * 128 - 1:(q + 1) * 128]
                k.act(cols[:, 8:9], mu_new, AF.Exp, bias=cols[:, hd:hd + 1], scale=-1.0)
                k.act(cols[:, 9:10], mu_new, AF.Exp, bias=mu_st, scale=-1.0)
                if _sub <= 2:
                    continue
                k.act(wrow, Mbc[:, qs], AF.Exp, bias=mu_st, scale=-1.0)
                if _sub <= 3:
                    continue
                k.tt("dve", tmpD, maskneg, Mbc[:, qs], ALU.subtract)
                k.act(Dt_, tmpD, AF.Exp, bias=cols[:, hd:hd + 1])
                if _stage <= 4:
                    continue
                for dt in range(4):
                    j = hd * 4 + dt
                    ds = slice(dt * 128, (dt + 1) * 128)
                    k.mm(ps[2][:, ds], BD[0][:, j, :], xcbm[j][:, qs])
                    k.mm(ps[3][:, ds], BD[1][:, j, :], xcbm[j][:, qs])
                    k.mm(ps[4][:, ds], xcbm[j][:, qs], BD[1][:, j, :])
                    k.mm(ps[5][:, ds], xmb[j][:, qs], BD[2][:, j, :])
                k.copy("act", qTb, r3(ps[2], 4, 128))
                k.tt("dve", qwb, qTb, wrow.re(wrow.ap.unsqueeze(1).to_broadcast([128, 4, 128])), ALU.mult)
                k.act(kTb, r3(ps[3], 4, 128), AF.Identity, scale=KS)
                k.ts("dve", cols[:, 14:15], cols[:, 8:9], KS, None, op0=ALU.mult)
                k.act(ktw, ps[4], AF.Identity, scale=cols[:, 14:15])
                k.copy("act", vtm, ps[5])
                if _stage <= 5:
                    continue
                for dt in range(4):
                    k.mm(ps[6][:, 0:128], kTb[:, dt, :], qTb[:, dt, :], start=(dt == 0), stop=(dt == 3))
                k.tt("dve", smat, ps[6][:, 0:128], Dt_, ALU.mult)
                for dt in range(4):
                    k.mm(ps[1], qwb[:, dt, :], Cb[:, dt, :], start=(dt == 0), stop=False)
                k.mm(ps[1], smat, vtm, start=False, stop=True)
                for dt in range(4):
                    k.mm(ps[6][:, 128:129], qwb[:, dt, :], nb[:, hd * 4 + dt, 0:1], start=(dt == 0), stop=False)
                k.mm(ps[6][:, 128:129], smat, ones_bf[:, 0:1], start=False, stop=True)
                k.act(cols[:, 10:11], ps[6][:, 128:129], AF.Abs)
                k.tt("dve", cols[:, 10:11], cols[:, 10:11], cols[:, 4 + hd:5 + hd], ALU.max)
                k.recip(cols[:, 10:11], cols[:, 10:11])
                if _stage <= 6:
                    continue
                k.act(hs, ps[1], AF.Identity, scale=cols[:, 10:11], accum=cols[:, 11:12])
                k.ts("dve", cols[:, 11:12], cols[:, 11:12], -1.0 / 512, None, op0=ALU.mult)
                k.ts("dve", hs, hs, cols[:, 11:12], None, op0=ALU.add)
                k.act(tD, hs, AF.Square, accum=cols[:, 12:13])
                k.act(cols[:, 12:13], cols[:, 12:13], AF.Sqrt, bias=g.eps_col[:, 0:1], scale=1.0 / 512)
                k.recip(cols[:, 12:13], cols[:, 12:13])
                k.ts("dve", hs, hs, cols[:, 12:13], None, op0=ALU.mult)
                for dt in range(4):
                    k.transpose(ps[7][:, dt * 128:(dt + 1) * 128], hs[:, dt * 128:(dt + 1) * 128], ident)
                for dt in range(4):
                    j = hd * 4 + dt
                    k.ts("dve", fm[j][:, qs], ps[7][:, dt * 128:(dt + 1) * 128], mng[:, j:j + 1], None, op0=ALU.mult)
                if _stage <= 7:
                    continue
                for dt in range(4):
                    ds = slice(dt * 128, (dt + 1) * 128)
                    pc = ps[2 + dt]
                    k.mm(pc, ktw[:, ds], vtm)
                    k.mm(ps[6][:, 130 + dt:131 + dt], ktw[:, ds], ones_bf[:, 0:1])
                    k.stt("dve", Cf[:, dt, :], Cf[:, dt, :], cols[:, 9:10], pc, ALU.mult, ALU.add)
                    k.copy("act", Cb[:, dt, :], Cf[:, dt, :])
                k.stt("dve", nf[:, hd * 4:hd * 4 + 4], nf[:, hd * 4:hd * 4 + 4], cols[:, 9:10], ps[6][:, 130:134],
                      ALU.mult, ALU.add)
                k.copy("act", nb[:, hd * 4:hd * 4 + 4, 0], nf[:, hd * 4:hd * 4 + 4])
                k.copy("act", mu_rep[:, hd:hd + 1], mu_new)
            k.dma("sp", cs_v, Cf)
        k.spacer = None
        for j in range(16):
            if j % 2 == 0:
                k.dma("pool", wgu_ring[3], w_up_d.re(w_up_d.ap[0, :, 2048 + j * 128:2048 + (j + 2) * 128]
                                                     .rearrange("(kt p) n -> p kt n", p=128)))
            proj(ps[j % 2], wgu_ring[3], (j % 2) * 128)
            k.act(tA, ps[j % 2], AF.Silu)
            k.stt("dve", tB, xcbm[j], msk[:, j:j + 1], fm[j], ALU.mult, ALU.add)
            k.tt("dve", hid[j], tB, tA, ALU.mult)
        for m in range(8):
            wo = wd_ring[1]
            k.dma("pool", wo[:, 0:16, :], w_dn_d.re(w_dn_d.ap[0, :, m * 128:(m + 1) * 128]
                                                    .rearrange("(kt p) n -> p kt n", p=128)))
            pd = ps[4 + m % 2]
            for kt in range(16):
                k.mm(pd, wo[:, kt, :], hid[kt], start=(kt == 0), stop=(kt == 15))
            k.stt("dve", xT[m], pd, g.gm_col(1, 1, m), xT[m], ALU.mult, ALU.add)

    return [hybrid, mlstm]


_NC_CACHE = {}


def kernel(**inputs):
    x = np.asarray(inputs["x"])
    B, S, _ = x.shape
    T = 512
    key = (S, T)
    if key not in _NC_CACHE:
        _NC_CACHE[key] = build(S, T, n_sub=6, final=True)
    nc = _NC_CACHE[key]
    in_maps = [make_inputs(inputs, b, S) for b in range(B)]
    res = run_bass_kernel_spmd(nc, in_maps, core_ids=list(range(B)))
    out = np.stack([np.asarray(r["outT"]).T for r in res.results], axis=0)
    return np.ascontiguousarray(out.astype(np.float32))
```

```python
import numpy as np
from contextlib import ExitStack
import concourse.bass as bass
import concourse.mybir as mybir
from concourse.bass_utils import run_bass_kernel_spmd

F32 = mybir.dt.float32
BF16 = mybir.dt.bfloat16
AF = mybir.ActivationFunctionType
ALU = mybir.AluOpType
AX = mybir.AxisListType


class Trk:
    __slots__ = ("w", "r", "dsem", "dcnt", "name", "dram")

    def __init__(self, name="", dram=False):
        self.dram = dram
        self.w = None
        self.r = []
        self.dsem = None
        self.dcnt = 0
        self.name = name


class V:
    __slots__ = ("ap", "trk")

    def __init__(self, ap, trk=None, name=""):
        self.ap = ap
        self.trk = trk if trk is not None else Trk(name)

    def __getitem__(self, key):
        return V(self.ap[key], self.trk)

    def re(self, ap):
        return V(ap, self.trk)


class Op:
    __slots__ = ("eng", "fn", "deps", "needed", "ticket", "dma", "trk")

    def __init__(self, eng, fn, dma=False, trk=None):
        self.eng = eng
        self.fn = fn
        self.deps = []
        self.needed = False
        self.ticket = None
        self.dma = dma
        self.trk = trk


class Rec:
    def __init__(self, nc, stack):
        self.nc = nc
        self.stack = stack
        self.ops = []
        self.E = {"pe": nc.tensor, "act": nc.scalar, "dve": nc.vector, "pool": nc.gpsimd, "sp": nc.sync}
        self.sems = {e: stack.enter_context(nc.semaphore("s_" + e)) for e in ("pe", "act", "dve", "pool")}
        self.nsem = 4
        self.out_ops = []
        self.spacer = None

    def op(self, eng, fn, w=(), r=(), dma=False):
        w = [x.trk if isinstance(x, V) else x for x in w]
        r = [x.trk if isinstance(x, V) else x for x in r]
        semtrk = None
        if dma:
            semtrk = r[0] if w[0].dram else w[0]
        w = [t for t in w if not t.dram]
        r = [t for t in r if not t.dram]
        o = Op(eng, fn, dma=dma, trk=semtrk)
        deps = {}
        for t in r:
            if t.w is not None:
                deps[id(t.w)] = (t.w, "raw")
        for t in w:
            if t.w is not None and id(t.w) not in deps:
                deps[id(t.w)] = (t.w, "waw")
            for ro in t.r:
                if id(ro) not in deps:
                    deps[id(ro)] = (ro, "war")
        for d, kind in deps.values():
            if d is o:
                continue
            if d.eng == eng and not d.dma and not dma:
                if eng == "pe":
                    continue
            o.deps.append(d)
            d.needed = True
        for t in r:
            t.r.append(o)
        for t in w:
            t.w = o
            t.r = []
        if dma:
            o.needed = True
            t = semtrk
            if t.dsem is None:
                t.dsem = self.stack.enter_context(self.nc.semaphore("d%d" % self.nsem))
                self.nsem += 1
        self.ops.append(o)
        if self.spacer is not None and eng in self.spacer and not dma:
            self.ops.append(Op(eng, self.spacer[eng]))
        return o

    def emit(self):
        cnt = {e: 0 for e in self.sems}
        seen = {e: {} for e in self.E}
        for o in self.ops:
            E = self.E[o.eng]
            sn = seen[o.eng]
            for d in o.deps:
                sem, val = d.ticket
                if sn.get(sem.num, 0) >= val:
                    continue
                E.wait_ge(sem, val)
                sn[sem.num] = val
            ins = o.fn()
            if o.dma:
                t = o.trk
                t.dcnt += 16
                ins.then_inc(t.dsem, 16)
                o.ticket = (t.dsem, t.dcnt)
            elif o.needed:
                cnt[o.eng] += 1
                ins.then_inc(self.sems[o.eng], 1)
                o.ticket = (self.sems[o.eng], cnt[o.eng])
        for o in self.out_ops:
            sem, val = o.ticket
            self.nc.sync.wait_ge(sem, val)

    def dma(self, eng, out, in_, is_out=False, **kw):
        E = self.E[eng]
        o = self.op(eng, lambda: E.dma_start(out=out.ap, in_=in_.ap, **kw), w=[out], r=[in_], dma=True)
        if is_out:
            self.out_ops.append(o)
        return o

    def mm(self, out, lhsT, rhs, start=True, stop=True, **kw):
        nc = self.nc
        return self.op("pe", lambda: nc.tensor.matmul(out.ap, lhsT.ap, rhs.ap, start=start, stop=stop, **kw),
                       w=[out], r=[lhsT, rhs])

    def transpose(self, out, in_, ident):
        nc = self.nc
        return self.op("pe", lambda: nc.tensor.transpose(out.ap, in_.ap, ident.ap), w=[out], r=[in_, ident])

    def act(self, out, in_, func, bias=None, scale=None, accum=None, eng="act"):
        nc = self.nc
        r = [in_]
        kw = {}
        if bias is not None:
            if isinstance(bias, V):
                kw["bias"] = bias.ap
                r.append(bias)
            else:
                kw["bias"] = bias
        if scale is not None:
            if isinstance(scale, V):
                kw["scale"] = scale.ap
                r.append(scale)
            else:
                kw["scale"] = scale
        w = [out]
        if accum is not None:
            kw["accum_out"] = accum.ap
            w.append(accum)
        return self.op("act", lambda: nc.scalar.activation(out=out.ap, in_=in_.ap, func=func, **kw), w=w, r=r)

    def tt(self, eng, out, in0, in1, op):
        E = self.E[eng]
        return self.op(eng, lambda: E.tensor_tensor(out=out.ap, in0=in0.ap, in1=in1.ap, op=op), w=[out], r=[in0, in1])

    def ts(self, eng, out, in0, s1, s2=None, op0=ALU.mult, op1=None, accum=None):
        E = self.E[eng]
        r = [in0]
        a1 = s1.ap if isinstance(s1, V) else s1
        a2 = s2.ap if isinstance(s2, V) else s2
        if isinstance(s1, V):
            r.append(s1)
        if isinstance(s2, V):
            r.append(s2)
        kw = {}
        if op1 is not None:
            kw["op1"] = op1
        w = [out]
        if accum is not None:
            kw["accum_out"] = accum.ap
            w.append(accum)
        return self.op(eng, lambda: E.tensor_scalar(out=out.ap, in0=in0.ap, scalar1=a1, scalar2=a2, op0=op0, **kw),
                       w=w, r=r)

    def stt(self, eng, out, in0, scalar, in1, op0, op1):
        E = self.E[eng]
        r = [in0, in1]
        a = scalar.ap if isinstance(scalar, V) else scalar
        if isinstance(scalar, V):
            r.append(scalar)
        return self.op(eng, lambda: E.scalar_tensor_tensor(out=out.ap, in0=in0.ap, scalar=a, in1=in1.ap,
                                                            op0=op0, op1=op1), w=[out], r=r)

    def copy(self, eng, out, in_):
        E = self.E[eng]
        if eng == "act":
            return self.op(eng, lambda: E.copy(out=out.ap, in_=in_.ap), w=[out], r=[in_])
        return self.op(eng, lambda: E.tensor_copy(out=out.ap, in_=in_.ap), w=[out], r=[in_])

    def memset(self, eng, out, val):
        E = self.E[eng]
        return self.op(eng, lambda: E.memset(out.ap, val), w=[out], r=[])

    def scan(self, eng, out, d0, d1, init, op0, op1):
        E = self.E[eng]
        r = [d0, d1]
        a = init.ap if isinstance(init, V) else init
        if isinstance(init, V):
            r.append(init)
        return self.op(eng, lambda: E.tensor_tensor_scan(out=out.ap, data0=d0.ap, data1=d1.ap, initial=a,
                                                         op0=op0, op1=op1), w=[out], r=r)

    def recip(self, out, in_):
        nc = self.nc
        return self.op("dve", lambda: nc.vector.reciprocal(out=out.ap, in_=in_.ap), w=[out], r=[in_])


D = 1024
DFF = 2816
KT = D // 128
FT = DFF // 128
EPS = 1e-6
HYB_IN = 4624


def col_layout(v):
    v = np.asarray(v, dtype=np.float32).reshape(-1, 128)
    return np.ascontiguousarray(v.T)


class Ctx:
    pass


def build(S, T, n_sub=6, final=True, dbg=False):
    assert S % T == 0 and T % 512 == 0
    NT = T // 512
    nc = bass.Bass("TRN2", target_bir_lowering=False)
    g = Ctx()
    g.nc = nc

    def din(name, shape, dt=F32):
        return V(nc.dram_tensor(name, list(shape), dt, kind="ExternalInput").ap(), Trk(name, dram=True))

    xT_d = din("xT", [D, S])
    outT_d = V(nc.dram_tensor("outT", [D, S], F32, kind="ExternalOutput").ap(), Trk("outT", dram=True))
    c_d = din("c_col", [128, KT])
    ada_w_d = din("ada_w", [2, D, 9 * D])
    ada_b_d = din("ada_b_col", [128, 2 * 72])
    ng_d = din("norm_g_col", [128, 2 * 3 * KT])
    fng_d = din("final_g_col", [128, KT])
    wg_d = din("ffn_w_gate", [2, 2, D, DFF])
    wu_d = din("ffn_w_up", [2, 2, D, DFF])
    wd_d = din("ffn_w_down", [2, 2, DFF, D])
    ident_d = din("ident", [128, 128])

    with ExitStack() as st:
        k = Rec(nc, st)
        g.k = k

        def sb(name, shape, dt=F32):
            return V(st.enter_context(nc.sbuf_tensor("sb_" + name, list(shape), dt))[:], name=name)

        def sbs(name, n, shape, dt=F32):
            t = st.enter_context(nc.sbuf_tensor("sb_" + name, [128, n] + list(shape), dt))
            return [V(t[:, i], name="%s%d" % (name, i)) for i in range(n)]

        ps = [V(st.enter_context(nc.psum_tensor("ps%d" % i, [128, 512], F32))[:], name="ps%d" % i) for i in range(8)]

        ones32 = sb("ones32", [128, 128])
        k.memset("dve", ones32, 1.0)
        eps_col = sb("eps_col", [128, 1])
        k.memset("dve", eps_col, EPS)
        ident = sb("ident", [128, 128])
        k.dma("sp", ident, ident_d)

        c_col = sb("c_col", [128, KT])
        k.dma("sp", c_col, c_d)
        cact = sb("cact", [128, KT])
        k.act(cact, c_col, AF.Silu)
        ada_b = sb("ada_b", [128, 144])
        k.dma("sp", ada_b, ada_b_d)
        ng = sb("ng", [128, 48])
        k.dma("sp", ng, ng_d)
        fng = sb("fng", [128, KT])
        k.dma("sp", fng, fng_d)
        mod = sb("mod", [128, 144])
        gs = sb("gs", [128, 48])
        gm = sb("gm", [128, 48])
        wa_yT = st.enter_context(nc.sbuf_tensor("sb_wa_yT", [128, KT, 512], F32))
        wa_ring = [V(wa_yT[:, 4 * i:4 * i + 4, :].rearrange("p a (b c) -> p (a b) c", b=2), name="wa%d" % i)
                   for i in range(2)]
        for layer in range(2):
            for cg in range(36):
                wa = wa_ring[cg % 2]
                k.dma("sp", wa, ada_w_d.re(ada_w_d.ap[layer, :, cg * 256:(cg + 1) * 256]
                                           .rearrange("(kt p) n -> p kt n", p=128)))
                for mi in range(2):
                    q = cg * 2 + mi
                    for kt in range(KT):
                        k.mm(ps[7][:, q:q + 1], wa[:, kt, mi * 128:(mi + 1) * 128], cact[:, kt:kt + 1],
                             start=(kt == 0), stop=(kt == KT - 1))
            k.tt("dve", mod[:, layer * 72:(layer + 1) * 72], ps[7][:, 0:72], ada_b[:, layer * 72:(layer + 1) * 72],
                 ALU.add)
            for sub in range(3):
                b0 = layer * 72 + sub * 24
                o0 = layer * 24 + sub * 8
                k.stt("dve", gs[:, o0:o0 + 8], mod[:, b0 + 8:b0 + 16], 1.0, ng[:, o0:o0 + 8], ALU.add, ALU.mult)
                k.ts("dve", gm[:, o0:o0 + 8], mod[:, b0 + 16:b0 + 24], 1.0, 1.0 if sub == 1 else 0.5,
                     op0=ALU.add, op1=ALU.mult)

        def sh_col(layer, sub, kt):
            c = layer * 72 + sub * 24 + kt
            return mod[:, c:c + 1]

        def gs_col(layer, sub, kt):
            c = layer * 24 + sub * 8 + kt
            return gs[:, c:c + 1]

        def gm_col(layer, sub, kt):
            c = layer * 24 + sub * 8 + kt
            return gm[:, c:c + 1]

        xT = sbs("xT", KT, [T])
        h = sb("h", [128, KT, T], BF16)
        sq = sbs("sq", 2, [512])
        tmpn = sbs("tmpn", KT, [512])
        rstd = sb("rstd", [128, 512])
        hid = sbs("hid", FT, [T], BF16)
        sg_ring = sbs("sg", 2, [512])
        wgu_ring = sbs("wgu", 4, [KT, 256], BF16)
        wd_ring = sbs("wd", 2, [FT, 128], BF16)
        outt = sbs("outt", KT, [T])
        g.cnt = 0

        def rms_stats(src_tiles, tb):
            sl = slice(tb * 512, (tb + 1) * 512)
            nk = len(src_tiles)
            for kt in range(nk):
                k.act(sq[kt % 2], src_tiles[kt][:, sl], AF.Square)
                k.mm(ps[6], ones32, sq[kt % 2], start=(kt == 0), stop=(kt == nk - 1))
            k.act(rstd, ps[6], AF.Sqrt, bias=eps_col[:, 0:1], scale=1.0 / (128 * nk))
            k.recip(rstd, rstd)

        def norm_mod(layer, sub):
            for tb in range(NT):
                sl = slice(tb * 512, (tb + 1) * 512)
                rms_stats(xT, tb)
                for kt in range(KT):
                    k.stt("dve", tmpn[kt], xT[kt][:, sl], gs_col(layer, sub, kt), rstd, ALU.mult, ALU.mult)
                    k.act(h[:, kt, sl], tmpn[kt], AF.Identity, bias=sh_col(layer, sub, kt))

        def ffn(layer, j):
            sub = 0 if j == 0 else 2
            norm_mod(layer, sub)
            for cg in range(11):
                wg = wgu_ring[(g.cnt % 2) * 2]
                wu = wgu_ring[(g.cnt % 2) * 2 + 1]
                g.cnt += 1
                for wt, wdram in ((wg, wg_d), (wu, wu_d)):
                    k.dma("pool", wt, wdram.re(wdram.ap[layer, j, :, cg * 256:(cg + 1) * 256]
                                               .rearrange("(kt p) n -> p kt n", p=128)))
                for mi in range(2):
                    m = cg * 2 + mi
                    for tb in range(NT):
                        sl = slice(tb * 512, (tb + 1) * 512)
                        pg = ps[(m * NT + tb) % 2]
                        pu = ps[2 + (m * NT + tb) % 2]
                        for kt in range(KT):
                            k.mm(pg, wg[:, kt, mi * 128:(mi + 1) * 128], h[:, kt, sl],
                                 start=(kt == 0), stop=(kt == KT - 1))
                        for kt in range(KT):
                            k.mm(pu, wu[:, kt, mi * 128:(mi + 1) * 128], h[:, kt, sl],
                                 start=(kt == 0), stop=(kt == KT - 1))
                        sgt = sg_ring[(m * NT + tb) % 2]
                        k.act(sgt, pg, AF.Silu)
                        k.tt("dve", hid[m][:, sl], sgt, pu, ALU.mult)
            for cg in range(8):
                wd = wd_ring[g.cnt % 2]
                g.cnt += 1
                k.dma("pool", wd, wd_d.re(wd_d.ap[layer, j, :, cg * 128:(cg + 1) * 128]
                                          .rearrange("(kt p) n -> p kt n", p=128)))
                for mi in range(1):
                    m = cg
                    for tb in range(NT):
                        sl = slice(tb * 512, (tb + 1) * 512)
                        pd = ps[4 + (m * NT + tb) % 2]
                        for kt in range(FT):
                            k.mm(pd, wd[:, kt, mi * 128:(mi + 1) * 128], hid[kt][:, sl],
                                 start=(kt == 0), stop=(kt == FT - 1))
                        k.stt("dve", xT[m][:, sl], pd, gm_col(layer, sub, m), xT[m][:, sl], ALU.mult, ALU.add)

        def final_norm(t0):
            for tb in range(NT):
                sl = slice(tb * 512, (tb + 1) * 512)
                rms_stats(xT, tb)
                for kt in range(KT):
                    k.stt("dve", outt[kt][:, sl], xT[kt][:, sl], fng[:, kt:kt + 1], rstd, ALU.mult, ALU.mult)
            for kt in range(KT):
                k.dma("sp", outT_d.re(outT_d.ap[kt * 128:(kt + 1) * 128, t0:t0 + T]), outt[kt], is_out=True)

        g.__dict__.update({kk: vv for kk, vv in locals().items() if kk != 'g'})
        mixers = make_mixers(g) if n_sub > 1 else None

        for ci in range(S // T):
            t0 = ci * T
            for kt in range(KT):
                k.dma("sp", xT[kt], xT_d.re(xT_d.ap[kt * 128:(kt + 1) * 128, t0:t0 + T]))
            si = 0
            for layer in range(2):
                for sub in range(3):
                    if si >= n_sub:
                        break
                    if sub == 1:
                        mixers[layer](ci, t0)
                    else:
                        ffn(layer, sub // 2)
                    si += 1
            if final:
                final_norm(t0)
            else:
                for kt in range(KT):
                    k.dma("sp", outT_d.re(outT_d.ap[kt * 128:(kt + 1) * 128, t0:t0 + T]), xT[kt], is_out=True)
        g.sbuf_left = nc.sbuf_bytes_remaining
        print('[build] sbuf bytes left/partition:', g.sbuf_left, 'sems:', k.nsem, 'ops:', len(k.ops))
        k.emit()
    return nc


def make_inputs(inp, b, S):
    f = lambda a: np.ascontiguousarray(np.asarray(a, dtype=np.float32))
    m = {
        "xT": f(np.asarray(inp["x"])[b, :S].T),
        "c_col": col_layout(inp["c"][b]),
        "ada_w": f(inp["ada_w"]),
        "ada_b_col": col_layout(inp["ada_b"]),
        "norm_g_col": col_layout(inp["norm_g"]),
        "final_g_col": col_layout(inp["final_norm_g"]),
        "ffn_w_gate": f(inp["ffn_w_gate"]),
        "ffn_w_up": f(inp["ffn_w_up"]),
        "ffn_w_down": f(inp["ffn_w_down"]),
        "ident": np.eye(128, dtype=np.float32),
        "tri": np.triu(np.ones((128, 128), dtype=np.float32)),
        "Umat": np.tril(np.ones((128, 128), dtype=np.float32), -1),
        "hyb_w_in": f(inp["hyb_w_in"]),
        "hyb_w_out": f(inp["hyb_w_out"]),
        "lru_wa": f(inp["lru_wa"]),
        "lru_wx": f(inp["lru_wx"]),
        "lru_conv_w_col": col_layout(inp["lru_conv_w"]),
        "lru_conv_b_col": col_layout(inp["lru_conv_b"]),
        "lru_ba_col": col_layout(inp["lru_ba"]),
        "lru_bx_col": col_layout(inp["lru_bx"]),
        "lru_lambda_col": col_layout(inp["lru_lambda"]),
        "ssd_conv_w_col": col_layout(inp["ssd_conv_w"]),
        "ssd_conv_b_col": col_layout(inp["ssd_conv_b"]),
        "ssd_dt_bias_row": f(np.tile(np.asarray(inp["ssd_dt_bias"]).reshape(1, 16), (128, 1))),
        "ssd_a_log_row": f(np.tile(np.asarray(inp["ssd_a_log"]).reshape(1, 16), (128, 1))),
        "ssd_d_col": col_layout(np.repeat(np.asarray(inp["ssd_d"]).reshape(16), 64)),
        "ssd_norm_g_col": col_layout(inp["ssd_norm_g"]),
        "mlstm_w_up": f(inp["mlstm_w_up"]),
        "mlstm_w_down": f(inp["mlstm_w_down"]),
        "bd_q": block_diag_layout(inp["mlstm_wq"]),
        "bd_k": block_diag_layout(inp["mlstm_wk"]),
        "bd_v": block_diag_layout(inp["mlstm_wv"]),
        "mlstm_w_gates": f(inp["mlstm_w_gates"]),
        "sel4": f(np.kron(np.eye(4, dtype=np.float32), np.ones((1, 128), dtype=np.float32))),
        "maskneg": f(np.where(np.triu(np.ones((128, 128))) > 0, 0.0, -30000.0)),
        "mlstm_conv_w_col": col_layout(inp["mlstm_conv_w"]),
        "mlstm_conv_b_col": col_layout(inp["mlstm_conv_b"]),
        "mlstm_norm_g_col": col_layout(inp["mlstm_norm_g"]),
        "mlstm_skip_col": col_layout(inp["mlstm_skip"]),
        "mlstm_bi_col": pad_col(np.asarray(inp["mlstm_b_gates"]).reshape(8)[0:4]),
        "mlstm_bf_col": pad_col(np.asarray(inp["mlstm_b_gates"]).reshape(8)[4:8]),
    }
    return m


def pad_col(v):
    o = np.zeros((128, 1), dtype=np.float32)
    o[:len(v), 0] = v
    return o


def block_diag_layout(w):
    w = np.asarray(w, dtype=np.float32).reshape(16, 32, 4, 4)
    o = np.zeros((16, 32, 4, 32, 4), dtype=np.float32)
    for b in range(32):
        o[:, b, :, b, :] = w[:, b]
    return np.ascontiguousarray(o.reshape(16, 128, 128))


def make_mixers(g):
    nc, k, st = g.nc, g.k, g.st
    sb, sbs, ps, din = g.sb, g.sbs, g.ps, g.din
    T = g.T
    assert T == 512
    xT, h, hid, tmpn, outt, rstd = g.xT, g.h, g.hid, g.tmpn, g.outt, g.rstd
    ident, ones32 = g.ident, g.ones32
    wgu_ring, wd_ring = g.wgu_ring, g.wd_ring
    NQ = T // 128

    tri_d = din("tri", [128, 128])
    U_d = din("Umat", [128, 128])
    tri = sb("tri", [128, 128])
    Um = sb("Um", [128, 128])
    k.dma("sp", tri, tri_d)
    k.dma("sp", Um, U_d)
    one_col = sb("one_col", [128, 1])
    k.memset("dve", one_col, 1.0)
    ps7q = [V(ps[7].ap[:, i * 128:(i + 1) * 128], name="ps7q%d" % i) for i in range(4)]

    def small(name, ncols):
        d = din(name, [128, ncols])
        t = sb("c_" + name, [128, ncols])
        k.dma("sp", t, d)
        return t

    def bc(v, a, b):
        return v.re(v.ap.unsqueeze(2).to_broadcast([128, a, b]))

    def r3(v, a, b):
        return v.re(v.ap.rearrange("p (a b) -> p a b", a=a))

    w_in_d = din("hyb_w_in", [1, D, HYB_IN])
    w_out_d = din("hyb_w_out", [1, 2048, D])
    lwa_d = din("lru_wa", [1, 8, 128, 128])
    lwx_d = din("lru_wx", [1, 8, 128, 128])
    lcw = small("lru_conv_w_col", 32)
    lcb = small("lru_conv_b_col", 8)
    lba = small("lru_ba_col", 8)
    lbx = small("lru_bx_col", 8)
    lam = small("lru_lambda_col", 8)
    scw = small("ssd_conv_w_col", 48)
    scb = small("ssd_conv_b_col", 12)
    dtb_row = small("ssd_dt_bias_row", 16)
    alog_row = small("ssd_a_log_row", 16)
    dcol = small("ssd_d_col", 8)
    sng = small("ssd_norm_g_col", 8)
    wa = sb("lwa", [128, 8, 128], BF16)
    wx = sb("lwx", [128, 8, 128], BF16)
    k.dma("pool", wa, lwa_d.re(lwa_d.ap[0].rearrange("n i o -> i n o")))
    k.dma("pool", wx, lwx_d.re(lwx_d.ap[0].rearrange("n i o -> i n o")))
    wdt = sb("wdt", [128, KT, 16], BF16)
    k.dma("pool", wdt, w_in_d.re(w_in_d.ap[0, :, 4608:4624].rearrange("(kt p) n -> p kt n", p=128)))
    clam = sb("clam", [128, 8])
    k.act(clam, lam, AF.Exp, scale=-1.0)
    k.ts("dve", clam, clam, 1.0, None, op0=ALU.add)
    k.act(clam, clam, AF.Ln)
    k.ts("dve", clam, clam, -8.0, None, op0=ALU.mult)
    ea_row = sb("ea_row", [128, 16])
    k.act(ea_row, alog_row, AF.Exp)
    lru_carry = sb("lru_carry", [128, 8, 3])
    ssd_carry = sb("ssd_carry", [128, 12, 3])
    lru_state = sb("lru_state", [128, 8])
    STf = sb("STf", [128, 1024])
    STb = sb("STb", [128, 1024], BF16)
    for t_ in (lru_carry, ssd_carry, lru_state, STf):
        k.memset("pool", t_, 0.0)
    k.memset("pool", STb, 0.0)
    tA = sb("tA", [128, T]); tB = sb("tB", [128, T]); tC = sb("tC", [128, T]); tD = sb("tD", [128, T])
    gl = sb("gl", [128, T]); xc = sb("xc", [128, T]); xcb = sb("xcb", [128, T], BF16)
    xpre = sbs("xpre", 2, [T + 3])
    zs = tmpn
    xbc = outt + sbs("xbc", 4, [T])
    bc16 = sbs("bc16", 4, [T], BF16)
    yT = [V(g.wa_yT[:, j, :], name="yT%d" % j) for j in range(8)]
    xdt = sb("xdt", [128, 1024], BF16); xdtd = sb("xdtd", [128, 1024], BF16)
    btm = sb("btm", [128, 256], BF16)
    cbm = sb("cbm", [128, 2, 128])
    Lring = sbs("Lr", 2, [128]); Ering = sbs("Er", 2, [128]); MTring = sbs("MTr", 2, [128], BF16)
    yoff = sb("yoff", [128, 512]); ytm = sb("ytm", [128, 1024])
    dtt = sb("dtt", [128, 16]); ac = sb("ac", [128, 16]); acs = sb("acs", [128, 16])
    dout = sb("dout", [128, 16]); dst = sb("dst", [128, 16]); dtot = sb("dtot", [128, 16])
    ycat = hid

    def conv(xp, j, cw, cb, ncw):
        k.ts("dve", xc, xp[:, 0:T], cw[:, j:j + 1], cb[:, j:j + 1], op0=ALU.mult, op1=ALU.add)
        for kk in range(1, 4):
            k.stt("dve", xc, xp[:, kk:kk + T], cw[:, kk * ncw + j:kk * ncw + j + 1], xc, ALU.mult, ALU.add)

    def proj(pst, wt, c0):
        for kt in range(KT):
            k.mm(pst, wt[:, kt, c0:c0 + 128], h[:, kt, :], start=(kt == 0), stop=(kt == KT - 1))

    def load_w(dram, col0, ncol=256):
        wt = wgu_ring[g.cnt % 4]
        g.cnt += 1
        k.dma("pool", wt[:, :, 0:ncol], dram.re(dram.ap[0, :, col0:col0 + ncol].rearrange("(kt p) n -> p kt n", p=128)))
        return wt

    def hybrid(ci, t0):
        g.norm_mod(0, 1)
        for hp in range(4):
            wgt = load_w(w_in_d, hp * 256)
            wxt = load_w(w_in_d, 1024 + hp * 256)
            for jj in range(2):
                j = hp * 2 + jj
                pgate, px = ps[0], ps[1]
                proj(pgate, wgt, jj * 128)
                proj(px, wxt, jj * 128)
                k.act(tA, pgate, AF.Square)
                k.ts("dve", tA, tA, 0.044715, 1.0, op0=ALU.mult, op1=ALU.add)
                k.tt("dve", tA, tA, pgate, ALU.mult)
                k.act(tA, tA, AF.Sigmoid, scale=1.5957691216057308)
                k.tt("dve", gl, tA, pgate, ALU.mult)
                xp = xpre[j % 2]
                k.copy("pool", xp[:, 0:3], lru_carry[:, j, :])
                k.copy("act", xp[:, 3:3 + T], px)
                k.copy("pool", lru_carry[:, j, :], xp[:, T:T + 3])
                conv(xp, j, lcw, lcb, 8)
                k.copy("act", xcb, xc)
                k.mm(ps[2], wa[:, j, :], xcb)
                k.mm(ps[3], wx[:, j, :], xcb)
                k.act(tB, ps[2], AF.Sigmoid, bias=lba[:, j:j + 1])
                k.act(tC, ps[3], AF.Sigmoid, bias=lbx[:, j:j + 1])
                k.act(tB, tB, AF.Exp, scale=clam[:, j:j + 1])
                k.tt("pool", tD, tB, tB, ALU.mult)
                k.act(tD, tD, AF.Sqrt, bias=one_col[:, 0:1], scale=-1.0)
                k.tt("pool", tC, tC, xc, ALU.mult)
                k.tt("dve", tC, tC, tD, ALU.mult)
                k.scan("dve", tD, tB, tC, lru_state[:, j:j + 1], ALU.mult, ALU.add)
                k.copy("pool", lru_state[:, j:j + 1], tD[:, T - 1:T])
                k.tt("dve", ycat[j], tD, gl, ALU.mult)
        for hp in range(4):
            wzt = load_w(w_in_d, 2048 + hp * 256)
            for jj in range(2):
                j = hp * 2 + jj
                proj(ps[j % 2], wzt, jj * 128)
                k.act(zs[j], ps[j % 2], AF.Silu)
        for hp in range(6):
            wbt = load_w(w_in_d, 3072 + hp * 256)
            for jj in range(2):
                j = hp * 2 + jj
                proj(ps[j % 2], wbt, jj * 128)
                xp = xpre[j % 2]
                k.copy("pool", xp[:, 0:3], ssd_carry[:, j, :])
                k.copy("act", xp[:, 3:3 + T], ps[j % 2])
                k.copy("pool", ssd_carry[:, j, :], xp[:, T:T + 3])
                conv(xp, j, scw, scb, 12)
                k.act(xbc[j], xc, AF.Silu)
                if j >= 8:
                    k.copy("pool", bc16[j - 8], xbc[j])
        for q in range(NQ):
            qs = slice(q * 128, (q + 1) * 128)
            for kt in range(KT):
                k.mm(ps[0][:, 256:272], h[:, kt, qs], wdt[:, kt, :], start=(kt == 0), stop=(kt == KT - 1))
            k.tt("dve", dtt, ps[0][:, 256:272], dtb_row, ALU.add)
            k.act(dtt, dtt, AF.Exp)
            k.ts("dve", dtt, dtt, 1.0, None, op0=ALU.add)
            k.act(dtt, dtt, AF.Ln)
            k.stt("dve", ac, dtt, -1.0, ea_row, ALU.mult, ALU.mult)
            k.mm(ps[0][:, 272:288], tri, ac)
            k.mm(ps[0][:, 288:304], ones32, ac)
            k.copy("act", acs, ps[0][:, 272:288])
            k.act(dout, acs, AF.Exp)
            k.tt("dve", dst, ps[0][:, 288:304], acs, ALU.subtract)
            k.act(dst, dst, AF.Exp)
            k.act(dtot, ps[0][:, 288:304], AF.Exp)
            for j in range(8):
                k.transpose(ps[2 + j // 4][:, (j % 4) * 128:(j % 4 + 1) * 128], xbc[j][:, qs], ident)
            for half in range(2):
                hs = slice(half * 512, (half + 1) * 512)
                k.tt("dve", r3(xdt[:, hs], 8, 64), r3(ps[2 + half], 8, 64), bc(dtt[:, half * 8:half * 8 + 8], 8, 64),
                     ALU.mult)
                k.tt("pool", r3(xdtd[:, hs], 8, 64), r3(xdt[:, hs], 8, 64), bc(dst[:, half * 8:half * 8 + 8], 8, 64),
                     ALU.mult)
            for gi in range(2):
                k.transpose(ps[0][:, gi * 128:(gi + 1) * 128], xbc[8 + gi][:, qs], ident)
            k.copy("act", btm, ps[0][:, 0:256])
            for gi in range(2):
                k.mm(ps[1][:, gi * 128:(gi + 1) * 128], bc16[gi][:, qs], bc16[2 + gi][:, qs])
            k.tt("dve", cbm, r3(ps[1][:, 0:256], 2, 128),
                 tri.re(tri.ap.unsqueeze(1).to_broadcast([128, 2, 128])), ALU.mult)
            for gi in range(2):
                gs_ = slice(gi * 512, (gi + 1) * 512)
                for e in range(8):
                    hh = gi * 8 + e
                    Lh = Lring[hh % 2]
                    k.ts("dve", Lh, Um, ac[:, hh:hh + 1], None, op0=ALU.mult)
                    pq = ps7q[hh % 4]
                    k.mm(pq, Lh, tri)
                    Eh = Ering[hh % 2]
                    k.act(Eh, pq, AF.Exp)
                    MT = MTring[hh % 2]
                    k.tt("dve", MT, Eh, cbm[:, gi, :], ALU.mult)
                    k.mm(ps[4 + gi][:, e * 64:(e + 1) * 64], MT, xdt[:, hh * 64:(hh + 1) * 64])
                k.mm(ps[6], bc16[2 + gi][:, qs], STb[:, gs_])
                k.tt("dve", r3(yoff, 8, 64), r3(ps[6], 8, 64), bc(dout[:, gi * 8:gi * 8 + 8], 8, 64), ALU.mult)
                k.tt("dve", ytm[:, gs_], ps[4 + gi], yoff, ALU.add)
                k.mm(ps[6], btm[:, gi * 128:(gi + 1) * 128], xdtd[:, gs_])
                k.tt("pool", r3(STf[:, gs_], 8, 64), r3(STf[:, gs_], 8, 64), bc(dtot[:, gi * 8:gi * 8 + 8], 8, 64),
                     ALU.mult)
                k.tt("dve", STf[:, gs_], STf[:, gs_], ps[6], ALU.add)
                k.copy("act", STb[:, gs_], STf[:, gs_])
            for j in range(8):
                k.transpose(ps[2 + j // 4][:, (j % 4) * 128:(j % 4 + 1) * 128], ytm[:, j * 128:(j + 1) * 128], ident)
            for j in range(8):
                k.stt("dve", yT[j][:, qs], xbc[j][:, qs], dcol[:, j:j + 1],
                      ps[2 + j // 4][:, (j % 4) * 128:(j % 4 + 1) * 128], ALU.mult, ALU.add)
        for j in range(8):
            k.tt("pool", yT[j], yT[j], zs[j], ALU.mult)
        for gi in range(2):
            g.rms_stats(yT[gi * 4:(gi + 1) * 4], 0)
            for jj in range(4):
                j = gi * 4 + jj
                k.stt("dve", ycat[8 + j], yT[j], sng[:, j:j + 1], rstd, ALU.mult, ALU.mult)
        for m in range(8):
            wo = wd_ring[g.cnt % 2]
            g.cnt += 1
            k.dma("pool", wo[:, 0:16, :], w_out_d.re(w_out_d.ap[0, :, m * 128:(m + 1) * 128]
                                                     .rearrange("(kt p) n -> p kt n", p=128)))
            pd = ps[4 + m % 2]
            for kt in range(16):
                k.mm(pd, wo[:, kt, :], ycat[kt], start=(kt == 0), stop=(kt == 15))
            k.stt("dve", xT[m], pd, g.gm_col(0, 1, m), xT[m], ALU.mult, ALU.add)

    w_up_d = din("mlstm_w_up", [1, D, 4096])
    w_dn_d = din("mlstm_w_down", [1, 2048, D])
    bdq_d = din("bd_q", [16, 128, 128])
    bdk_d = din("bd_k", [16, 128, 128])
    bdv_d = din("bd_v", [16, 128, 128])
    wgt_d = din("mlstm_w_gates", [1, 6144, 8])
    sel_d = din("sel4", [4, 512])
    mneg_d = din("maskneg", [128, 128])
    cst_d = V(nc.dram_tensor("c_state", [4, 512, 512], F32, kind="ExternalOutput").ap(), name="c_state")
    mcw = small("mlstm_conv_w_col", 64)
    mcb = small("mlstm_conv_b_col", 16)
    mng = small("mlstm_norm_g_col", 16)
    msk = small("mlstm_skip_col", 16)
    bgi = small("mlstm_bi_col", 1)
    bgf = small("mlstm_bf_col", 1)
    maskneg = sb("maskneg", [128, 128])
    k.dma("sp", maskneg, mneg_d)
    wgi = sb("wgi", [128, 48, 4], BF16)
    wgf = sb("wgf", [128, 48, 4], BF16)
    k.dma("pool", wgi, wgt_d.re(wgt_d.ap[0, :, 0:4].rearrange("(kt p) n -> p kt n", p=128)))
    k.dma("pool", wgf, wgt_d.re(wgt_d.ap[0, :, 4:8].rearrange("(kt p) n -> p kt n", p=128)))
    ones_bf = sb("ones_bf", [128, 1], BF16)
    k.memset("dve", ones_bf, 1.0)
    ml_carry = sb("ml_carry", [128, 16, 3])
    nf = sb("nf", [128, 16])
    nb = sb("nb", [128, 16, 2], BF16)
    mu_rep = sb("mu_rep", [128, 4])
    Fc = sb("Fc", [128, 1])
    Mc = sb("Mc", [128, 1])
    for t_ in (ml_carry, nf, mu_rep, Fc, Mc):
        k.memset("pool", t_, 0.0)
    k.memset("pool", nb, 0.0)
    Cf = sb("Cf", [128, 4, 512])
    xmb_x = sb("xmb_x", [128, T], BF16)
    qkvt = sb("qkvt", [128, T], BF16)
    qTb = sb("qTb", [128, 4, 128], BF16); qwb = sb("qwb", [128, 4, 128], BF16); kTb = sb("kTb", [128, 4, 128], BF16)
    ktw = sb("ktw", [128, 512], BF16); vtm = sb("vtm", [128, 512], BF16)
    cols = sb("mlcols", [128, 16])
    xcbm = hid[0:16]
    xmb = hid[16:22] + bc16 + [xcb, xdt[:, 0:512], xdt[:, 512:1024], xdtd[:, 0:512], xdtd[:, 512:1024], xmb_x]
    fm = tmpn + xbc[0:8]
    ipre, fl, Fr, gg, Mr, emt, onesT, sel4 = [yT[i] for i in range(8)]
    BD = [wgu_ring[i].re(wgu_ring[i].ap.rearrange("p a (b c) -> p (a b) c", b=2)) for i in range(3)]
    Cb = wgu_ring[3].re(wgu_ring[3].ap.rearrange("p (a b) c -> p a (b c)", b=2))
    wz_t = wd_ring[0].re(wd_ring[0].ap[:, 0:16, :].rearrange("p (a b) c -> p a (b c)", b=2))
    tmpD, wrow = Lring[0], Lring[1]
    Dt_, smat, hs = Ering[0], MTring[0], tC
    KS = 512 ** -0.5
    spc = sb("spc", [128, 128])
    spacer = {"act": lambda: nc.scalar.copy(out=spc.ap, in_=ones32.ap),
              "dve": lambda: nc.vector.tensor_copy(out=spc.ap, in_=ones32.ap)}

    def mlstm(ci, t0):
        g.norm_mod(1, 1)
        for i, d_ in enumerate((bdq_d, bdk_d, bdv_d)):
            k.dma("pool", BD[i], d_.re(d_.ap.rearrange("n i o -> i n o")))
        k.dma("sp", sel4[0:4, :], sel_d)
        k.memset("pool", onesT[0:4, :], 1.0)
        import os as _os
        _stage = int(_os.environ.get("MLSTM_STAGE", "9"))
        if _stage <= 0:
            return
        for j in range(16):
            wt = wgu_ring[3]
            if j % 2 == 0:
                k.dma("pool", wt, w_up_d.re(w_up_d.ap[0, :, j * 128:(j + 2) * 128]
                                            .rearrange("(kt p) n -> p kt n", p=128)))
            px = ps[j % 2]
            proj(px, wt, (j % 2) * 128)
            xp = xpre[j % 2]
            k.copy("pool", xp[:, 0:3], ml_carry[:, j, :])
            k.copy("act", xp[:, 3:3 + T], px)
            k.copy("pool", ml_carry[:, j, :], xp[:, T:T + 3])
            k.copy("act", xmb[j], px)
            conv(xp, j, mcw, mcb, 16)
            k.act(xcbm[j], xc, AF.Silu)
            for wi, (src, bdm) in enumerate(((xcbm[j], BD[0]), (xcbm[j], BD[1]), (xmb[j], BD[2]))):
                if _os.environ.get("MLSTM_NOBD"):
                    continue
                k.mm(ps[2 + wi % 2], bdm[:, j, :], src)
                k.copy("act", qkvt, ps[2 + wi % 2])
                first = (j == 0 and wi == 0)
                last = (j == 15 and wi == 2)
                if _os.environ.get("MLSTM_NOGATE"):
                    continue
                k.mm(ps[7][0:4, :], wgi[:, wi * 16 + j, :], qkvt, start=first, stop=last)
                k.mm(ps[6][0:4, :], wgf[:, wi * 16 + j, :], qkvt, start=first, stop=last)
        import os as _os
        _stage = int(_os.environ.get("MLSTM_STAGE", "9"))
        if _stage <= 1:
            return
        I4 = slice(0, 4)
        k.act(ipre[I4, :], ps[7][0:4, :], AF.Identity, bias=bgi[0:4, 0:1])
        k.act(fl[I4, :], ps[6][0:4, :], AF.Identity, bias=bgf[0:4, 0:1])
        k.act(fl[I4, :], fl[I4, :], AF.Exp, scale=-1.0)
        k.ts("dve", fl[I4, :], fl[I4, :], 1.0, None, op0=ALU.add)
        k.act(fl[I4, :], fl[I4, :], AF.Ln)
        k.ts("dve", fl[I4, :], fl[I4, :], -1.0, None, op0=ALU.mult)
        k.scan("dve", Fr[I4, :], onesT[I4, :], fl[I4, :], Fc[0:4, 0:1], ALU.mult, ALU.add)
        k.tt("dve", gg[I4, :], ipre[I4, :], Fr[I4, :], ALU.subtract)
        k.scan("dve", Mr[I4, :], gg[I4, :], gg[I4, :], Mc[0:4, 0:1], ALU.max, ALU.max)
        k.copy("pool", Fc[0:4, 0:1], Fr[I4, T - 1:T])
        k.copy("pool", Mc[0:4, 0:1], Mr[I4, T - 1:T])
        k.tt("dve", emt[I4, :], Fr[I4, :], Mr[I4, :], ALU.add)
        k.act(emt[I4, :], emt[I4, :], AF.Exp, scale=-1.0)
        if _stage <= 2:
            return
        for hd in range(4):
            cs_v = cst_d.re(cst_d.ap[hd].rearrange("(dt p) v -> p dt v", p=128))
            if ci == 0:
                k.memset("pool", Cf, 0.0)
            else:
                k.dma("sp", Cf, cs_v)
            k.copy("act", Cb, Cf)
            k.mm(ps[0], sel4[0:4, hd * 128:(hd + 1) * 128], Mr[I4, :])
            Mbc = tA
            k.copy("act", Mbc, ps[0])
            for q in range(NQ):
                if _stage <= 3:
                    continue
                qs = slice(q * 128, (q + 1) * 128)
                k.transpose(ps[6][:, 136:140], gg[I4, qs], ident[0:4, 0:4])
                k.transpose(ps[6][:, 140:144], emt[I4, qs], ident[0:4, 0:4])
                k.copy("act", cols[:, 0:8], ps[6][:, 136:144])
                _sub = int(_os.environ.get("MLSTM_SUB", "9"))
                if _sub <= 1:
                    continue
                mu_st = mu_rep[:, hd:hd + 1]
                mu_new = Mbc[:, (q + 1) * 128 - 1:(q + 1) * 128]
                k.act(cols[:, 8:9], mu_new, AF.Exp, bias=cols[:, hd:hd + 1], scale=-1.0)
                k.act(cols[:, 9:10], mu_new, AF.Exp, bias=mu_st, scale=-1.0)
                if _sub <= 2:
                    continue
                k.act(wrow, Mbc[:, qs], AF.Exp, bias=mu_st, scale=-1.0)
                if _sub <= 3:
                    continue
                k.tt("dve", tmpD, maskneg, Mbc[:, qs], ALU.subtract)
                k.act(Dt_, tmpD, AF.Exp, bias=cols[:, hd:hd + 1])
                if _stage <= 4:
                    continue
                for dt in range(4):
                    j = hd * 4 + dt
                    ds = slice(dt * 128, (dt + 1) * 128)
                    k.mm(ps[2][:, ds], BD[0][:, j, :], xcbm[j][:, qs])
                    k.mm(ps[3][:, ds], BD[1][:, j, :], xcbm[j][:, qs])
                    k.mm(ps[4][:, ds], xcbm[j][:, qs], BD[1][:, j, :])
                    k.mm(ps[5][:, ds], xmb[j][:, qs], BD[2][:, j, :])
                k.copy("act", qTb, r3(ps[2], 4, 128))
                k.tt("dve", qwb, qTb, wrow.re(wrow.ap.unsqueeze(1).to_broadcast([128, 4, 128])), ALU.mult)
                k.act(kTb, r3(ps[3], 4, 128), AF.Identity, scale=KS)
                k.ts("dve", cols[:, 14:15], cols[:, 8:9], KS, None, op0=ALU.mult)
                k.act(ktw, ps[4], AF.Identity, scale=cols[:, 14:15])
                k.copy("act", vtm, ps[5])
                if _stage <= 5:
                    continue
                for dt in range(4):
                    k.mm(ps[6][:, 0:128], kTb[:, dt, :], qTb[:, dt, :], start=(dt == 0), stop=(dt == 3))
                k.tt("dve", smat, ps[6][:, 0:128], Dt_, ALU.mult)
                for dt in range(4):
                    k.mm(ps[1], qwb[:, dt, :], Cb[:, dt, :], start=(dt == 0), stop=False)
                k.mm(ps[1], smat, vtm, start=False, stop=True)
                for dt in range(4):
                    k.mm(ps[6][:, 128:129], qwb[:, dt, :], nb[:, hd * 4 + dt, 0:1], start=(dt == 0), stop=False)
                k.mm(ps[6][:, 128:129], smat, ones_bf[:, 0:1], start=False, stop=True)
                k.act(cols[:, 10:11], ps[6][:, 128:129], AF.Abs)
                k.tt("dve", cols[:, 10:11], cols[:, 10:11], cols[:, 4 + hd:5 + hd], ALU.max)
                k.recip(cols[:, 10:11], cols[:, 10:11])
                if _stage <= 6:
                    continue
                k.act(hs, ps[1], AF.Identity, scale=cols[:, 10:11], accum=cols[:, 11:12])
                k.ts("dve", cols[:, 11:12], cols[:, 11:12], -1.0 / 512, None, op0=ALU.mult)
                k.ts("dve", hs, hs, cols[:, 11:12], None, op0=ALU.add)
                k.act(tD, hs, AF.Square, accum=cols[:, 12:13])
                k.act(cols[:, 12:13], cols[:, 12:13], AF.Sqrt, bias=g.eps_col[:, 0:1], scale=1.0 / 512)
                k.recip(cols[:, 12:13], cols[:, 12:13])
                k.ts("dve", hs, hs, cols[:, 12:13], None, op0=ALU.mult)
                for dt in range(4):
                    k.transpose(ps[7][:, dt * 128:(dt + 1) * 128], hs[:, dt * 128:(dt + 1) * 128], ident)
                for dt in range(4):
                    j = hd * 4 + dt
                    k.ts("dve", fm[j][:, qs], ps[7][:, dt * 128:(dt + 1) * 128], mng[:, j:j + 1], None, op0=ALU.mult)
                if _stage <= 7:
                    continue
                for dt in range(4):
                    ds = slice(dt * 128, (dt + 1) * 128)
                    pc = ps[2 + dt]
                    k.mm(pc, ktw[:, ds], vtm)
                    k.mm(ps[6][:, 130 + dt:131 + dt], ktw[:, ds], ones_bf[:, 0:1])
                    k.stt("dve", Cf[:, dt, :], Cf[:, dt, :], cols[:, 9:10], pc, ALU.mult, ALU.add)
                    k.copy("act", Cb[:, dt, :], Cf[:, dt, :])
                k.stt("dve", nf[:, hd * 4:hd * 4 + 4], nf[:, hd * 4:hd * 4 + 4], cols[:, 9:10], ps[6][:, 130:134],
                      ALU.mult, ALU.add)
                k.copy("act", nb[:, hd * 4:hd * 4 + 4, 0], nf[:, hd * 4:hd * 4 + 4])
                k.copy("act", mu_rep[:, hd:hd + 1], mu_new)
            k.dma("sp", cs_v, Cf)
        k.spacer = None
        for j in range(16):
            if j % 2 == 0:
                k.dma("pool", wgu_ring[3], w_up_d.re(w_up_d.ap[0, :, 2048 + j * 128:2048 + (j + 2) * 128]
                                                     .rearrange("(kt p) n -> p kt n", p=128)))
            proj(ps[j % 2], wgu_ring[3], (j % 2) * 128)
            k.act(tA, ps[j % 2], AF.Silu)
            k.stt("dve", tB, xcbm[j], msk[:, j:j + 1], fm[j], ALU.mult, ALU.add)
            k.tt("dve", hid[j], tB, tA, ALU.mult)
        for m in range(8):
            wo = wd_ring[1]
            k.dma("pool", wo[:, 0:16, :], w_dn_d.re(w_dn_d.ap[0, :, m * 128:(m + 1) * 128]
                                                    .rearrange("(kt p) n -> p kt n", p=128)))
            pd = ps[4 + m % 2]
            for kt in range(16):
                k.mm(pd, wo[:, kt, :], hid[kt], start=(kt == 0), stop=(kt == 15))
            k.stt("dve", xT[m], pd, g.gm_col(1, 1, m), xT[m], ALU.mult, ALU.add)

    return [hybrid, mlstm]


_NC_CACHE = {}


def kernel(**inputs):
    x = np.asarray(inputs["x"])
    B, S, _ = x.shape
    T = 512
    key = (S, T)
    if key not in _NC_CACHE:
        _NC_CACHE[key] = build(S, T, n_sub=6, final=True)
    nc = _NC_CACHE[key]
    in_maps = [make_inputs(inputs, b, S) for b in range(B)]
    res = run_bass_kernel_spmd(nc, in_maps, core_ids=list(range(B)))
    out = np.stack([np.asarray(r["outT"]).T for r in res.results], axis=0)
    return np.ascontiguousarray(out.astype(np.float32))
```

```python
import numpy as np
from contextlib import ExitStack
import concourse.bass as bass
import concourse.mybir as mybir
from concourse.bass_utils import run_bass_kernel_spmd

F32 = mybir.dt.float32
BF16 = mybir.dt.bfloat16
AF = mybir.ActivationFunctionType
ALU = mybir.AluOpType
AX = mybir.AxisListType


class Trk:
    __slots__ = ("w", "r", "dsem", "dcnt", "name", "dram")

    def __init__(self, name="", dram=False):
        self.dram = dram
        self.w = None
        self.r = []
        self.dsem = None
        self.dcnt = 0
        self.name = name


class V:
    __slots__ = ("ap", "trk")

    def __init__(self, ap, trk=None, name=""):
        self.ap = ap
        self.trk = trk if trk is not None else Trk(name)

    def __getitem__(self, key):
        return V(self.ap[key], self.trk)

    def re(self, ap):
        return V(ap, self.trk)


class Op:
    __slots__ = ("eng", "fn", "deps", "needed", "ticket", "dma", "trk")

    def __init__(self, eng, fn, dma=False, trk=None):
        self.eng = eng
        self.fn = fn
        self.deps = []
        self.needed = False
        self.ticket = None
        self.dma = dma
        self.trk = trk


class Rec:
    def __init__(self, nc, stack):
        self.nc = nc
        self.stack = stack
        self.ops = []
        self.E = {"pe": nc.tensor, "act": nc.scalar, "dve": nc.vector, "pool": nc.gpsimd, "sp": nc.sync}
        self.sems = {e: stack.enter_context(nc.semaphore("s_" + e)) for e in ("pe", "act", "dve", "pool")}
        self.nsem = 4
        self.out_ops = []
        self.spacer = None

    def op(self, eng, fn, w=(), r=(), dma=False):
        w = [x.trk if isinstance(x, V) else x for x in w]
        r = [x.trk if isinstance(x, V) else x for x in r]
        semtrk = None
        if dma:
            semtrk = r[0] if w[0].dram else w[0]
        w = [t for t in w if not t.dram]
        r = [t for t in r if not t.dram]
        o = Op(eng, fn, dma=dma, trk=semtrk)
        deps = {}
        for t in r:
            if t.w is not None:
                deps[id(t.w)] = (t.w, "raw")
        for t in w:
            if t.w is not None and id(t.w) not in deps:
                deps[id(t.w)] = (t.w, "waw")
            for ro in t.r:
                if id(ro) not in deps:
                    deps[id(ro)] = (ro, "war")
        for d, kind in deps.values():
            if d is o:
                continue
            if d.eng == eng and not d.dma and not dma:
                if eng == "pe":
                    continue
            o.deps.append(d)
            d.needed = True
        for t in r:
            t.r.append(o)
        for t in w:
            t.w = o
            t.r = []
        if dma:
            o.needed = True
            t = semtrk
            if t.dsem is None:
                t.dsem = self.stack.enter_context(self.nc.semaphore("d%d" % self.nsem))
                self.nsem += 1
        self.ops.append(o)
        if self.spacer is not None and eng in self.spacer and not dma:
            self.ops.append(Op(eng, self.spacer[eng]))
        return o

    def emit(self):
        cnt = {e: 0 for e in self.sems}
        seen = {e: {} for e in self.E}
        for o in self.ops:
            E = self.E[o.eng]
            sn = seen[o.eng]
            for d in o.deps:
                sem, val = d.ticket
                if sn.get(sem.num, 0) >= val:
                    continue
                E.wait_ge(sem, val)
                sn[sem.num] = val
            ins = o.fn()
            if o.dma:
                t = o.trk
                t.dcnt += 16
                ins.then_inc(t.dsem, 16)
                o.ticket = (t.dsem, t.dcnt)
            elif o.needed:
                cnt[o.eng] += 1
                ins.then_inc(self.sems[o.eng], 1)
                o.ticket = (self.sems[o.eng], cnt[o.eng])
        for o in self.out_ops:
            sem, val = o.ticket
            self.nc.sync.wait_ge(sem, val)

    def dma(self, eng, out, in_, is_out=False, **kw):
        E = self.E[eng]
        o = self.op(eng, lambda: E.dma_start(out=out.ap, in_=in_.ap, **kw), w=[out], r=[in_], dma=True)
        if is_out:
            self.out_ops.append(o)
        return o

    def mm(self, out, lhsT, rhs, start=True, stop=True, **kw):
        nc = self.nc
        return self.op("pe", lambda: nc.tensor.matmul(out.ap, lhsT.ap, rhs.ap, start=start, stop=stop, **kw),
                       w=[out], r=[lhsT, rhs])

    def transpose(self, out, in_, ident):
        nc = self.nc
        return self.op("pe", lambda: nc.tensor.transpose(out.ap, in_.ap, ident.ap), w=[out], r=[in_, ident])

    def act(self, out, in_, func, bias=None, scale=None, accum=None, eng="act"):
        nc = self.nc
        r = [in_]
        kw = {}
        if bias is not None:
            if isinstance(bias, V):
                kw["bias"] = bias.ap
                r.append(bias)
            else:
                kw["bias"] = bias
        if scale is not None:
            if isinstance(scale, V):
                kw["scale"] = scale.ap
                r.append(scale)
            else:
                kw["scale"] = scale
        w = [out]
        if accum is not None:
            kw["accum_out"] = accum.ap
            w.append(accum)
        return self.op("act", lambda: nc.scalar.activation(out=out.ap, in_=in_.ap, func=func, **kw), w=w, r=r)

    def tt(self, eng, out, in0, in1, op):
        E = self.E[eng]
        return self.op(eng, lambda: E.tensor_tensor(out=out.ap, in0=in0.ap, in1=in1.ap, op=op), w=[out], r=[in0, in1])

    def ts(self, eng, out, in0, s1, s2=None, op0=ALU.mult, op1=None, accum=None):
        E = self.E[eng]
        r = [in0]
        a1 = s1.ap if isinstance(s1, V) else s1
        a2 = s2.ap if isinstance(s2, V) else s2
        if isinstance(s1, V):
            r.append(s1)
        if isinstance(s2, V):
            r.append(s2)
        kw = {}
        if op1 is not None:
            kw["op1"] = op1
        w = [out]
        if accum is not None:
            kw["accum_out"] = accum.ap
            w.append(accum)
        return self.op(eng, lambda: E.tensor_scalar(out=out.ap, in0=in0.ap, scalar1=a1, scalar2=a2, op0=op0, **kw),
                       w=w, r=r)

    def stt(self, eng, out, in0, scalar, in1, op0, op1):
        E = self.E[eng]
        r = [in0, in1]
        a = scalar.ap if isinstance(scalar, V) else scalar
        if isinstance(scalar, V):
            r.append(scalar)
        return self.op(eng, lambda: E.scalar_tensor_tensor(out=out.ap, in0=in0.ap, scalar=a, in1=in1.ap,
                                                            op0=op0, op1=op1), w=[out], r=r)

    def copy(self, eng, out, in_):
        E = self.E[eng]
        if eng == "act":
            return self.op(eng, lambda: E.copy(out=out.ap, in_=in_.ap), w=[out], r=[in_])
        return self.op(eng, lambda: E.tensor_copy(out=out.ap, in_=in_.ap), w=[out], r=[in_])

    def memset(self, eng, out, val):
        E = self.E[eng]
        return self.op(eng, lambda: E.memset(out.ap, val), w=[out], r=[])

    def scan(self, eng, out, d0, d1, init, op0, op1):
        E = self.E[eng]
        r = [d0, d1]
        a = init.ap if isinstance(init, V) else init
        if isinstance(init, V):
            r.append(init)
        return self.op(eng, lambda: E.tensor_tensor_scan(out=out.ap, data0=d0.ap, data1=d1.ap, initial=a,
                                                         op0=op0, op1=op1), w=[out], r=r)

    def recip(self, out, in_):
        nc = self.nc
        return self.op("dve", lambda: nc.vector.reciprocal(out=out.ap, in_=in_.ap), w=[out], r=[in_])


D = 1024
DFF = 2816
KT = D // 128
FT = DFF // 128
EPS = 1e-6
HYB_IN = 4624


def col_layout(v):
    v = np.asarray(v, dtype=np.float32).reshape(-1, 128)
    return np.ascontiguousarray(v.T)


class Ctx:
    pass


def build(S, T, n_sub=6, final=True, dbg=False):
    assert S % T == 0 and T % 512 == 0
    NT = T // 512
    nc = bass.Bass("TRN2", target_bir_lowering=False)
    g = Ctx()
    g.nc = nc

    def din(name, shape, dt=F32):
        return V(nc.dram_tensor(name, list(shape), dt, kind="ExternalInput").ap(), Trk(name, dram=True))

    xT_d = din("xT", [D, S])
    outT_d = V(nc.dram_tensor("outT", [D, S], F32, kind="ExternalOutput").ap(), Trk("outT", dram=True))
    c_d = din("c_col", [128, KT])
    ada_w_d = din("ada_w", [2, D, 9 * D])
    ada_b_d = din("ada_b_col", [128, 2 * 72])
    ng_d = din("norm_g_col", [128, 2 * 3 * KT])
    fng_d = din("final_g_col", [128, KT])
    wg_d = din("ffn_w_gate", [2, 2, 11, 128, KT * 256])
    wu_d = din("ffn_w_up", [2, 2, 11, 128, KT * 256])
    wd_d = din("ffn_w_down", [2, 2, 8, 128, FT * 128])
    ident_d = din("ident", [128, 128])

    with ExitStack() as st:
        k = Rec(nc, st)
        g.k = k

        def sb(name, shape, dt=F32):
            return V(st.enter_context(nc.sbuf_tensor("sb_" + name, list(shape), dt))[:], name=name)

        def sbs(name, n, shape, dt=F32):
            t = st.enter_context(nc.sbuf_tensor("sb_" + name, [128, n] + list(shape), dt))
            return [V(t[:, i], name="%s%d" % (name, i)) for i in range(n)]

        ps = [V(st.enter_context(nc.psum_tensor("ps%d" % i, [128, 512], F32))[:], name="ps%d" % i) for i in range(8)]

        ones32 = sb("ones32", [128, 128])
        k.memset("dve", ones32, 1.0)
        eps_col = sb("eps_col", [128, 1])
        k.memset("dve", eps_col, EPS)
        ident = sb("ident", [128, 128])
        k.dma("sp", ident, ident_d)

        c_col = sb("c_col", [128, KT])
        k.dma("sp", c_col, c_d)
        cact = sb("cact", [128, KT])
        k.act(cact, c_col, AF.Silu)
        ada_b = sb("ada_b", [128, 144])
        k.dma("sp", ada_b, ada_b_d)
        ng = sb("ng", [128, 48])
        k.dma("sp", ng, ng_d)
        fng = sb("fng", [128, KT])
        k.dma("sp", fng, fng_d)
        mod = sb("mod", [128, 144])
        gs = sb("gs", [128, 48])
        gm = sb("gm", [128, 48])
        wa_yT = st.enter_context(nc.sbuf_tensor("sb_wa_yT", [128, KT, 512], F32))
        wa_ring = [V(wa_yT[:, 4 * i:4 * i + 4, :].rearrange("p a (b c) -> p (a b) c", b=2), name="wa%d" % i)
                   for i in range(2)]
        for layer in range(2):
            for cg in range(36):
                wa = wa_ring[cg % 2]
                k.dma("sp", wa, ada_w_d.re(ada_w_d.ap[layer, :, cg * 256:(cg + 1) * 256]
                                           .rearrange("(kt p) n -> p kt n", p=128)))
                for mi in range(2):
                    q = cg * 2 + mi
                    for kt in range(KT):
                        k.mm(ps[7][:, q:q + 1], wa[:, kt, mi * 128:(mi + 1) * 128], cact[:, kt:kt + 1],
                             start=(kt == 0), stop=(kt == KT - 1))
            k.tt("dve", mod[:, layer * 72:(layer + 1) * 72], ps[7][:, 0:72], ada_b[:, layer * 72:(layer + 1) * 72],
                 ALU.add)
            for sub in range(3):
                b0 = layer * 72 + sub * 24
                o0 = layer * 24 + sub * 8
                k.stt("dve", gs[:, o0:o0 + 8], mod[:, b0 + 8:b0 + 16], 1.0, ng[:, o0:o0 + 8], ALU.add, ALU.mult)
                k.ts("dve", gm[:, o0:o0 + 8], mod[:, b0 + 16:b0 + 24], 1.0, 1.0 if sub == 1 else 0.5,
                     op0=ALU.add, op1=ALU.mult)

        def sh_col(layer, sub, kt):
            c = layer * 72 + sub * 24 + kt
            return mod[:, c:c + 1]

        def gs_col(layer, sub, kt):
            c = layer * 24 + sub * 8 + kt
            return gs[:, c:c + 1]

        def gm_col(layer, sub, kt):
            c = layer * 24 + sub * 8 + kt
            return gm[:, c:c + 1]

        xT = sbs("xT", KT, [T])
        h = sb("h", [128, KT, T], BF16)
        sq = sbs("sq", 2, [512])
        tmpn = sbs("tmpn", KT, [512])
        rstd = sb("rstd", [128, 512])
        hid = sbs("hid", FT, [T], BF16)
        sg_ring = sbs("sg", 2, [512])
        wgu_ring = sbs("wgu", 4, [KT, 256], BF16)
        wd_ring = sbs("wd", 2, [FT, 128], BF16)
        outt = sbs("outt", KT, [T])
        g.cnt = 0

        def rms_stats(src_tiles, tb):
            sl = slice(tb * 512, (tb + 1) * 512)
            nk = len(src_tiles)
            for kt in range(nk):
                k.act(sq[kt % 2], src_tiles[kt][:, sl], AF.Square)
                k.mm(ps[6], ones32, sq[kt % 2], start=(kt == 0), stop=(kt == nk - 1))
            k.act(rstd, ps[6], AF.Sqrt, bias=eps_col[:, 0:1], scale=1.0 / (128 * nk))
            k.recip(rstd, rstd)

        def norm_mod(layer, sub):
            for tb in range(NT):
                sl = slice(tb * 512, (tb + 1) * 512)
                rms_stats(xT, tb)
                for kt in range(KT):
                    k.stt("dve", tmpn[kt], xT[kt][:, sl], gs_col(layer, sub, kt), rstd, ALU.mult, ALU.mult)
                    k.act(h[:, kt, sl], tmpn[kt], AF.Identity, bias=sh_col(layer, sub, kt))

        def ffn(layer, j):
            sub = 0 if j == 0 else 2
            norm_mod(layer, sub)
            for cg in range(11):
                wg = wgu_ring[(g.cnt % 2) * 2]
                wu = wgu_ring[(g.cnt % 2) * 2 + 1]
                g.cnt += 1
                for wt, wdram in ((wg, wg_d), (wu, wu_d)):
                    k.dma("pool", wt, wdram.re(wdram.ap[layer, j, cg].rearrange("p (kt n) -> p kt n", kt=KT)))
                for mi in range(2):
                    m = cg * 2 + mi
                    for tb in range(NT):
                        sl = slice(tb * 512, (tb + 1) * 512)
                        pg = ps[(m * NT + tb) % 2]
                        pu = ps[2 + (m * NT + tb) % 2]
                        for kt in range(KT):
                            k.mm(pg, wg[:, kt, mi * 128:(mi + 1) * 128], h[:, kt, sl],
                                 start=(kt == 0), stop=(kt == KT - 1))
                        for kt in range(KT):
                            k.mm(pu, wu[:, kt, mi * 128:(mi + 1) * 128], h[:, kt, sl],
                                 start=(kt == 0), stop=(kt == KT - 1))
                        sgt = sg_ring[(m * NT + tb) % 2]
                        k.act(sgt, pg, AF.Silu)
                        k.tt("dve", hid[m][:, sl], sgt, pu, ALU.mult)
            for cg in range(8):
                wd = wd_ring[g.cnt % 2]
                g.cnt += 1
                k.dma("pool", wd, wd_d.re(wd_d.ap[layer, j, cg].rearrange("p (kt n) -> p kt n", kt=FT)))
                for mi in range(1):
                    m = cg
                    for tb in range(NT):
                        sl = slice(tb * 512, (tb + 1) * 512)
                        pd = ps[4 + (m * NT + tb) % 2]
                        for kt in range(FT):
                            k.mm(pd, wd[:, kt, mi * 128:(mi + 1) * 128], hid[kt][:, sl],
                                 start=(kt == 0), stop=(kt == FT - 1))
                        k.stt("dve", xT[m][:, sl], pd, gm_col(layer, sub, m), xT[m][:, sl], ALU.mult, ALU.add)

        def final_norm(t0):
            for tb in range(NT):
                sl = slice(tb * 512, (tb + 1) * 512)
                rms_stats(xT, tb)
                for kt in range(KT):
                    k.stt("dve", outt[kt][:, sl], xT[kt][:, sl], fng[:, kt:kt + 1], rstd, ALU.mult, ALU.mult)
            for kt in range(KT):
                k.dma("sp", outT_d.re(outT_d.ap[kt * 128:(kt + 1) * 128, t0:t0 + T]), outt[kt], is_out=True)

        g.__dict__.update({kk: vv for kk, vv in locals().items() if kk != 'g'})
        mixers = make_mixers(g) if n_sub > 1 else None

        for ci in range(S // T):
            t0 = ci * T
            for kt in range(KT):
                k.dma("sp", xT[kt], xT_d.re(xT_d.ap[kt * 128:(kt + 1) * 128, t0:t0 + T]))
            si = 0
            for layer in range(2):
                for sub in range(3):
                    if si >= n_sub:
                        break
                    if sub == 1:
                        mixers[layer](ci, t0)
                    else:
                        ffn(layer, sub // 2)
                    si += 1
            if final:
                final_norm(t0)
            else:
                for kt in range(KT):
                    k.dma("sp", outT_d.re(outT_d.ap[kt * 128:(kt + 1) * 128, t0:t0 + T]), xT[kt], is_out=True)
        g.sbuf_left = nc.sbuf_bytes_remaining
        print('[build] sbuf bytes left/partition:', g.sbuf_left, 'sems:', k.nsem, 'ops:', len(k.ops))
        k.emit()
    return nc


def make_inputs(inp, b, S):
    f = lambda a: np.ascontiguousarray(np.asarray(a, dtype=np.float32))
    m = {
        "xT": f(np.asarray(inp["x"])[b, :S].T),
        "c_col": col_layout(inp["c"][b]),
        "ada_w": f(inp["ada_w"]),
        "ada_b_col": col_layout(inp["ada_b"]),
        "norm_g_col": col_layout(inp["norm_g"]),
        "final_g_col": col_layout(inp["final_norm_g"]),
        "ffn_w_gate": tile_gu(inp["ffn_w_gate"]),
        "ffn_w_up": tile_gu(inp["ffn_w_up"]),
        "ffn_w_down": tile_dn(inp["ffn_w_down"]),
        "ident": np.eye(128, dtype=np.float32),
        "tri": np.triu(np.ones((128, 128), dtype=np.float32)),
        "Umat": np.tril(np.ones((128, 128), dtype=np.float32), -1),
        "hyb_w_in": f(inp["hyb_w_in"]),
        "hyb_w_out": f(inp["hyb_w_out"]),
        "lru_wa": f(inp["lru_wa"]),
        "lru_wx": f(inp["lru_wx"]),
        "lru_conv_w_col": col_layout(inp["lru_conv_w"]),
        "lru_conv_b_col": col_layout(inp["lru_conv_b"]),
        "lru_ba_col": col_layout(inp["lru_ba"]),
        "lru_bx_col": col_layout(inp["lru_bx"]),
        "lru_lambda_col": col_layout(inp["lru_lambda"]),
        "ssd_conv_w_col": col_layout(inp["ssd_conv_w"]),
        "ssd_conv_b_col": col_layout(inp["ssd_conv_b"]),
        "ssd_dt_bias_row": f(np.tile(np.asarray(inp["ssd_dt_bias"]).reshape(1, 16), (128, 1))),
        "ssd_a_log_row": f(np.tile(np.asarray(inp["ssd_a_log"]).reshape(1, 16), (128, 1))),
        "ssd_d_col": col_layout(np.repeat(np.asarray(inp["ssd_d"]).reshape(16), 64)),
        "ssd_norm_g_col": col_layout(inp["ssd_norm_g"]),
        "mlstm_w_up": f(inp["mlstm_w_up"]),
        "mlstm_w_down": f(inp["mlstm_w_down"]),
        "bd_q": block_diag_layout(inp["mlstm_wq"]),
        "bd_k": block_diag_layout(inp["mlstm_wk"]),
        "bd_v": block_diag_layout(inp["mlstm_wv"]),
        "mlstm_w_gates": f(inp["mlstm_w_gates"]),
        "sel4": f(np.kron(np.eye(4, dtype=np.float32), np.ones((1, 128), dtype=np.float32))),
        "maskneg": f(np.where(np.triu(np.ones((128, 128))) > 0, 0.0, -30000.0)),
        "mlstm_conv_w_col": col_layout(inp["mlstm_conv_w"]),
        "mlstm_conv_b_col": col_layout(inp["mlstm_conv_b"]),
        "mlstm_norm_g_col": col_layout(inp["mlstm_norm_g"]),
        "mlstm_skip_col": col_layout(inp["mlstm_skip"]),
        "mlstm_bi_col": pad_col(np.asarray(inp["mlstm_b_gates"]).reshape(8)[0:4]),
        "mlstm_bf_col": pad_col(np.asarray(inp["mlstm_b_gates"]).reshape(8)[4:8]),
    }
    return m


_TILE_CACHE = {}


def tile_gu(w):
    key = ("gu", id(w))
    if key not in _TILE_CACHE:
        a = np.asarray(w, dtype=np.float32).reshape(2, 2, 8, 128, 11, 256)
        _TILE_CACHE[key] = (w, np.ascontiguousarray(a.transpose(0, 1, 4, 3, 2, 5)).reshape(2, 2, 11, 128, 2048))
    return _TILE_CACHE[key][1]


def tile_dn(w):
    key = ("dn", id(w))
    if key not in _TILE_CACHE:
        a = np.asarray(w, dtype=np.float32).reshape(2, 2, 22, 128, 8, 128)
        _TILE_CACHE[key] = (w, np.ascontiguousarray(a.transpose(0, 1, 4, 3, 2, 5)).reshape(2, 2, 8, 128, 2816))
    return _TILE_CACHE[key][1]


def pad_col(v):
    o = np.zeros((128, 1), dtype=np.float32)
    o[:len(v), 0] = v
    return o


def block_diag_layout(w):
    w = np.asarray(w, dtype=np.float32).reshape(16, 32, 4, 4)
    o = np.zeros((16, 32, 4, 32, 4), dtype=np.float32)
    for b in range(32):
        o[:, b, :, b, :] = w[:, b]
    return np.ascontiguousarray(o.reshape(16, 128, 128))


def make_mixers(g):
    nc, k, st = g.nc, g.k, g.st
    sb, sbs, ps, din = g.sb, g.sbs, g.ps, g.din
    T = g.T
    assert T == 512
    xT, h, hid, tmpn, outt, rstd = g.xT, g.h, g.hid, g.tmpn, g.outt, g.rstd
    ident, ones32 = g.ident, g.ones32
    wgu_ring, wd_ring = g.wgu_ring, g.wd_ring
    NQ = T // 128

    tri_d = din("tri", [128, 128])
    U_d = din("Umat", [128, 128])
    tri = sb("tri", [128, 128])
    Um = sb("Um", [128, 128])
    k.dma("sp", tri, tri_d)
    k.dma("sp", Um, U_d)
    one_col = sb("one_col", [128, 1])
    k.memset("dve", one_col, 1.0)
    ps7q = [V(ps[7].ap[:, i * 128:(i + 1) * 128], name="ps7q%d" % i) for i in range(4)]

    def small(name, ncols):
        d = din(name, [128, ncols])
        t = sb("c_" + name, [128, ncols])
        k.dma("sp", t, d)
        return t

    def bc(v, a, b):
        return v.re(v.ap.unsqueeze(2).to_broadcast([128, a, b]))

    def r3(v, a, b):
        return v.re(v.ap.rearrange("p (a b) -> p a b", a=a))

    w_in_d = din("hyb_w_in", [1, D, HYB_IN])
    w_out_d = din("hyb_w_out", [1, 2048, D])
    lwa_d = din("lru_wa", [1, 8, 128, 128])
    lwx_d = din("lru_wx", [1, 8, 128, 128])
    lcw = small("lru_conv_w_col", 32)
    lcb = small("lru_conv_b_col", 8)
    lba = small("lru_ba_col", 8)
    lbx = small("lru_bx_col", 8)
    lam = small("lru_lambda_col", 8)
    scw = small("ssd_conv_w_col", 48)
    scb = small("ssd_conv_b_col", 12)
    dtb_row = small("ssd_dt_bias_row", 16)
    alog_row = small("ssd_a_log_row", 16)
    dcol = small("ssd_d_col", 8)
    sng = small("ssd_norm_g_col", 8)
    wa = sb("lwa", [128, 8, 128], BF16)
    wx = sb("lwx", [128, 8, 128], BF16)
    k.dma("pool", wa, lwa_d.re(lwa_d.ap[0].rearrange("n i o -> i n o")))
    k.dma("pool", wx, lwx_d.re(lwx_d.ap[0].rearrange("n i o -> i n o")))
    wdt = sb("wdt", [128, KT, 16], BF16)
    k.dma("pool", wdt, w_in_d.re(w_in_d.ap[0, :, 4608:4624].rearrange("(kt p) n -> p kt n", p=128)))
    clam = sb("clam", [128, 8])
    k.act(clam, lam, AF.Exp, scale=-1.0)
    k.ts("dve", clam, clam, 1.0, None, op0=ALU.add)
    k.act(clam, clam, AF.Ln)
    k.ts("dve", clam, clam, -8.0, None, op0=ALU.mult)
    ea_row = sb("ea_row", [128, 16])
    k.act(ea_row, alog_row, AF.Exp)
    lru_carry = sb("lru_carry", [128, 8, 3])
    ssd_carry = sb("ssd_carry", [128, 12, 3])
    lru_state = sb("lru_state", [128, 8])
    STf = sb("STf", [128, 1024])
    STb = sb("STb", [128, 1024], BF16)
    for t_ in (lru_carry, ssd_carry, lru_state, STf):
        k.memset("pool", t_, 0.0)
    k.memset("pool", STb, 0.0)
    tA = sb("tA", [128, T]); tB = sb("tB", [128, T]); tC = sb("tC", [128, T]); tD = sb("tD", [128, T])
    gl = sb("gl", [128, T]); xc = sb("xc", [128, T]); xcb = sb("xcb", [128, T], BF16)
    xpre = sbs("xpre", 2, [T + 3])
    zs = tmpn
    xbc = outt + sbs("xbc", 4, [T])
    bc16 = sbs("bc16", 4, [T], BF16)
    yT = [V(g.wa_yT[:, j, :], name="yT%d" % j) for j in range(8)]
    xdt = sb("xdt", [128, 1024], BF16); xdtd = sb("xdtd", [128, 1024], BF16)
    btm = sb("btm", [128, 256], BF16)
    cbm = sb("cbm", [128, 2, 128])
    Lring = sbs("Lr", 2, [128]); Ering = sbs("Er", 2, [128]); MTring = sbs("MTr", 2, [128], BF16)
    yoff = sb("yoff", [128, 512]); ytm = sb("ytm", [128, 1024])
    dtt = sb("dtt", [128, 16]); ac = sb("ac", [128, 16]); acs = sb("acs", [128, 16])
    dout = sb("dout", [128, 16]); dst = sb("dst", [128, 16]); dtot = sb("dtot", [128, 16])
    ycat = hid

    def conv(xp, j, cw, cb, ncw):
        k.ts("dve", xc, xp[:, 0:T], cw[:, j:j + 1], cb[:, j:j + 1], op0=ALU.mult, op1=ALU.add)
        for kk in range(1, 4):
            k.stt("dve", xc, xp[:, kk:kk + T], cw[:, kk * ncw + j:kk * ncw + j + 1], xc, ALU.mult, ALU.add)

    def proj(pst, wt, c0):
        for kt in range(KT):
            k.mm(pst, wt[:, kt, c0:c0 + 128], h[:, kt, :], start=(kt == 0), stop=(kt == KT - 1))

    def load_w(dram, col0, ncol=256):
        wt = wgu_ring[g.cnt % 4]
        g.cnt += 1
        k.dma("pool", wt[:, :, 0:ncol], dram.re(dram.ap[0, :, col0:col0 + ncol].rearrange("(kt p) n -> p kt n", p=128)))
        return wt

    def hybrid(ci, t0):
        g.norm_mod(0, 1)
        for hp in range(4):
            wgt = load_w(w_in_d, hp * 256)
            wxt = load_w(w_in_d, 1024 + hp * 256)
            for jj in range(2):
                j = hp * 2 + jj
                pgate, px = ps[0], ps[1]
                proj(pgate, wgt, jj * 128)
                proj(px, wxt, jj * 128)
                k.act(tA, pgate, AF.Square)
                k.ts("dve", tA, tA, 0.044715, 1.0, op0=ALU.mult, op1=ALU.add)
                k.tt("dve", tA, tA, pgate, ALU.mult)
                k.act(tA, tA, AF.Sigmoid, scale=1.5957691216057308)
                k.tt("dve", gl, tA, pgate, ALU.mult)
                xp = xpre[j % 2]
                k.copy("pool", xp[:, 0:3], lru_carry[:, j, :])
                k.copy("act", xp[:, 3:3 + T], px)
                k.copy("pool", lru_carry[:, j, :], xp[:, T:T + 3])
                conv(xp, j, lcw, lcb, 8)
                k.copy("act", xcb, xc)
                k.mm(ps[2], wa[:, j, :], xcb)
                k.mm(ps[3], wx[:, j, :], xcb)
                k.act(tB, ps[2], AF.Sigmoid, bias=lba[:, j:j + 1])
                k.act(tC, ps[3], AF.Sigmoid, bias=lbx[:, j:j + 1])
                k.act(tB, tB, AF.Exp, scale=clam[:, j:j + 1])
                k.tt("pool", tD, tB, tB, ALU.mult)
                k.act(tD, tD, AF.Sqrt, bias=one_col[:, 0:1], scale=-1.0)
                k.tt("pool", tC, tC, xc, ALU.mult)
                k.tt("dve", tC, tC, tD, ALU.mult)
                k.scan("dve", tD, tB, tC, lru_state[:, j:j + 1], ALU.mult, ALU.add)
                k.copy("pool", lru_state[:, j:j + 1], tD[:, T - 1:T])
                k.tt("dve", ycat[j], tD, gl, ALU.mult)
        for hp in range(4):
            wzt = load_w(w_in_d, 2048 + hp * 256)
            for jj in range(2):
                j = hp * 2 + jj
                proj(ps[j % 2], wzt, jj * 128)
                k.act(zs[j], ps[j % 2], AF.Silu)
        for hp in range(6):
            wbt = load_w(w_in_d, 3072 + hp * 256)
            for jj in range(2):
                j = hp * 2 + jj
                proj(ps[j % 2], wbt, jj * 128)
                xp = xpre[j % 2]
                k.copy("pool", xp[:, 0:3], ssd_carry[:, j, :])
                k.copy("act", xp[:, 3:3 + T], ps[j % 2])
                k.copy("pool", ssd_carry[:, j, :], xp[:, T:T + 3])
                conv(xp, j, scw, scb, 12)
                k.act(xbc[j], xc, AF.Silu)
                if j >= 8:
                    k.copy("pool", bc16[j - 8], xbc[j])
        for q in range(NQ):
            qs = slice(q * 128, (q + 1) * 128)
            for kt in range(KT):
                k.mm(ps[0][:, 256:272], h[:, kt, qs], wdt[:, kt, :], start=(kt == 0), stop=(kt == KT - 1))
            k.tt("dve", dtt, ps[0][:, 256:272], dtb_row, ALU.add)
            k.act(dtt, dtt, AF.Exp)
            k.ts("dve", dtt, dtt, 1.0, None, op0=ALU.add)
            k.act(dtt, dtt, AF.Ln)
            k.stt("dve", ac, dtt, -1.0, ea_row, ALU.mult, ALU.mult)
            k.mm(ps[0][:, 272:288], tri, ac)
            k.mm(ps[0][:, 288:304], ones32, ac)
            k.copy("act", acs, ps[0][:, 272:288])
            k.act(dout, acs, AF.Exp)
            k.tt("dve", dst, ps[0][:, 288:304], acs, ALU.subtract)
            k.act(dst, dst, AF.Exp)
            k.act(dtot, ps[0][:, 288:304], AF.Exp)
            for j in range(8):
                k.transpose(ps[2 + j // 4][:, (j % 4) * 128:(j % 4 + 1) * 128], xbc[j][:, qs], ident)
            for half in range(2):
                hs = slice(half * 512, (half + 1) * 512)
                k.tt("dve", r3(xdt[:, hs], 8, 64), r3(ps[2 + half], 8, 64), bc(dtt[:, half * 8:half * 8 + 8], 8, 64),
                     ALU.mult)
                k.tt("pool", r3(xdtd[:, hs], 8, 64), r3(xdt[:, hs], 8, 64), bc(dst[:, half * 8:half * 8 + 8], 8, 64),
                     ALU.mult)
            for gi in range(2):
                k.transpose(ps[0][:, gi * 128:(gi + 1) * 128], xbc[8 + gi][:, qs], ident)
            k.copy("act", btm, ps[0][:, 0:256])
            for gi in range(2):
                k.mm(ps[1][:, gi * 128:(gi + 1) * 128], bc16[gi][:, qs], bc16[2 + gi][:, qs])
            k.tt("dve", cbm, r3(ps[1][:, 0:256], 2, 128),
                 tri.re(tri.ap.unsqueeze(1).to_broadcast([128, 2, 128])), ALU.mult)
            for gi in range(2):
                gs_ = slice(gi * 512, (gi + 1) * 512)
                for e in range(8):
                    hh = gi * 8 + e
                    Lh = Lring[hh % 2]
                    k.ts("dve", Lh, Um, ac[:, hh:hh + 1], None, op0=ALU.mult)
                    pq = ps7q[hh % 4]
                    k.mm(pq, Lh, tri)
                    Eh = Ering[hh % 2]
                    k.act(Eh, pq, AF.Exp)
                    MT = MTring[hh % 2]
                    k.tt("dve", MT, Eh, cbm[:, gi, :], ALU.mult)
                    k.mm(ps[4 + gi][:, e * 64:(e + 1) * 64], MT, xdt[:, hh * 64:(hh + 1) * 64])
                k.mm(ps[6], bc16[2 + gi][:, qs], STb[:, gs_])
                k.tt("dve", r3(yoff, 8, 64), r3(ps[6], 8, 64), bc(dout[:, gi * 8:gi * 8 + 8], 8, 64), ALU.mult)
                k.tt("dve", ytm[:, gs_], ps[4 + gi], yoff, ALU.add)
                k.mm(ps[6], btm[:, gi * 128:(gi + 1) * 128], xdtd[:, gs_])
                k.tt("pool", r3(STf[:, gs_], 8, 64), r3(STf[:, gs_], 8, 64), bc(dtot[:, gi * 8:gi * 8 + 8], 8, 64),
                     ALU.mult)
                k.tt("dve", STf[:, gs_], STf[:, gs_], ps[6], ALU.add)
                k.copy("act", STb[:, gs_], STf[:, gs_])
            for j in range(8):
                k.transpose(ps[2 + j // 4][:, (j % 4) * 128:(j % 4 + 1) * 128], ytm[:, j * 128:(j + 1) * 128], ident)
            for j in range(8):
                k.stt("dve", yT[j][:, qs], xbc[j][:, qs], dcol[:, j:j + 1],
                      ps[2 + j // 4][:, (j % 4) * 128:(j % 4 + 1) * 128], ALU.mult, ALU.add)
        for j in range(8):
            k.tt("pool", yT[j], yT[j], zs[j], ALU.mult)
        for gi in range(2):
            g.rms_stats(yT[gi * 4:(gi + 1) * 4], 0)
            for jj in range(4):
                j = gi * 4 + jj
                k.stt("dve", ycat[8 + j], yT[j], sng[:, j:j + 1], rstd, ALU.mult, ALU.mult)
        for m in range(8):
            wo = wd_ring[g.cnt % 2]
            g.cnt += 1
            k.dma("pool", wo[:, 0:16, :], w_out_d.re(w_out_d.ap[0, :, m * 128:(m + 1) * 128]
                                                     .rearrange("(kt p) n -> p kt n", p=128)))
            pd = ps[4 + m % 2]
            for kt in range(16):
                k.mm(pd, wo[:, kt, :], ycat[kt], start=(kt == 0), stop=(kt == 15))
            k.stt("dve", xT[m], pd, g.gm_col(0, 1, m), xT[m], ALU.mult, ALU.add)

    w_up_d = din("mlstm_w_up", [1, D, 4096])
    w_dn_d = din("mlstm_w_down", [1, 2048, D])
    bdq_d = din("bd_q", [16, 128, 128])
    bdk_d = din("bd_k", [16, 128, 128])
    bdv_d = din("bd_v", [16, 128, 128])
    wgt_d = din("mlstm_w_gates", [1, 6144, 8])
    sel_d = din("sel4", [4, 512])
    mneg_d = din("maskneg", [128, 128])
    cst_d = V(nc.dram_tensor("c_state", [4, 512, 512], F32, kind="ExternalOutput").ap(), name="c_state")
    mcw = small("mlstm_conv_w_col", 64)
    mcb = small("mlstm_conv_b_col", 16)
    mng = small("mlstm_norm_g_col", 16)
    msk = small("mlstm_skip_col", 16)
    bgi = small("mlstm_bi_col", 1)
    bgf = small("mlstm_bf_col", 1)
    maskneg = sb("maskneg", [128, 128])
    k.dma("sp", maskneg, mneg_d)
    wgi = sb("wgi", [128, 48, 4], BF16)
    wgf = sb("wgf", [128, 48, 4], BF16)
    k.dma("pool", wgi, wgt_d.re(wgt_d.ap[0, :, 0:4].rearrange("(kt p) n -> p kt n", p=128)))
    k.dma("pool", wgf, wgt_d.re(wgt_d.ap[0, :, 4:8].rearrange("(kt p) n -> p kt n", p=128)))
    ones_bf = sb("ones_bf", [128, 1], BF16)
    k.memset("dve", ones_bf, 1.0)
    ml_carry = sb("ml_carry", [128, 16, 3])
    nf = sb("nf", [128, 16])
    nb = sb("nb", [128, 16, 2], BF16)
    mu_rep = sb("mu_rep", [128, 4])
    Fc = sb("Fc", [128, 1])
    Mc = sb("Mc", [128, 1])
    for t_ in (ml_carry, nf, mu_rep, Fc, Mc):
        k.memset("pool", t_, 0.0)
    k.memset("pool", nb, 0.0)
    Cf = sb("Cf", [128, 4, 512])
    xmb_x = sb("xmb_x", [128, T], BF16)
    qkvt = sb("qkvt", [128, T], BF16)
    qTb = sb("qTb", [128, 4, 128], BF16); qwb = sb("qwb", [128, 4, 128], BF16); kTb = sb("kTb", [128, 4, 128], BF16)
    ktw = sb("ktw", [128, 512], BF16); vtm = sb("vtm", [128, 512], BF16)
    cols = sb("mlcols", [128, 16])
    xcbm = hid[0:16]
    xmb = hid[16:22] + bc16 + [xcb, xdt[:, 0:512], xdt[:, 512:1024], xdtd[:, 0:512], xdtd[:, 512:1024], xmb_x]
    fm = tmpn + xbc[0:8]
    ipre, fl, Fr, gg, Mr, emt, onesT, sel4 = [yT[i] for i in range(8)]
    BD = [wgu_ring[i].re(wgu_ring[i].ap.rearrange("p a (b c) -> p (a b) c", b=2)) for i in range(3)]
    Cb = wgu_ring[3].re(wgu_ring[3].ap.rearrange("p (a b) c -> p a (b c)", b=2))
    wz_t = wd_ring[0].re(wd_ring[0].ap[:, 0:16, :].rearrange("p (a b) c -> p a (b c)", b=2))
    tmpD, wrow = Lring[0], Lring[1]
    Dt_, smat, hs = Ering[0], MTring[0], tC
    KS = 512 ** -0.5
    spc = sb("spc", [128, 128])
    spacer = {"act": lambda: nc.scalar.copy(out=spc.ap, in_=ones32.ap),
              "dve": lambda: nc.vector.tensor_copy(out=spc.ap, in_=ones32.ap)}

    def mlstm(ci, t0):
        g.norm_mod(1, 1)
        for i, d_ in enumerate((bdq_d, bdk_d, bdv_d)):
            k.dma("pool", BD[i], d_.re(d_.ap.rearrange("n i o -> i n o")))
        k.dma("sp", sel4[0:4, :], sel_d)
        k.memset("pool", onesT[0:4, :], 1.0)
        import os as _os
        _stage = int(_os.environ.get("MLSTM_STAGE", "9"))
        if _stage <= 0:
            return
        for j in range(16):
            wt = wgu_ring[3]
            if j % 2 == 0:
                k.dma("pool", wt, w_up_d.re(w_up_d.ap[0, :, j * 128:(j + 2) * 128]
                                            .rearrange("(kt p) n -> p kt n", p=128)))
            px = ps[j % 2]
            proj(px, wt, (j % 2) * 128)
            xp = xpre[j % 2]
            k.copy("pool", xp[:, 0:3], ml_carry[:, j, :])
            k.copy("act", xp[:, 3:3 + T], px)
            k.copy("pool", ml_carry[:, j, :], xp[:, T:T + 3])
            k.copy("act", xmb[j], px)
            conv(xp, j, mcw, mcb, 16)
            k.act(xcbm[j], xc, AF.Silu)
            for wi, (src, bdm) in enumerate(((xcbm[j], BD[0]), (xcbm[j], BD[1]), (xmb[j], BD[2]))):
                if _os.environ.get("MLSTM_NOBD"):
                    continue
                k.mm(ps[2 + wi % 2], bdm[:, j, :], src)
                k.copy("act", qkvt, ps[2 + wi % 2])
                first = (j == 0 and wi == 0)
                last = (j == 15 and wi == 2)
                if _os.environ.get("MLSTM_NOGATE"):
                    continue
                k.mm(ps[7][0:4, :], wgi[:, wi * 16 + j, :], qkvt, start=first, stop=last)
                k.mm(ps[6][0:4, :], wgf[:, wi * 16 + j, :], qkvt, start=first, stop=last)
        import os as _os
        _stage = int(_os.environ.get("MLSTM_STAGE", "9"))
        if _stage <= 1:
            return
        I4 = slice(0, 4)
        k.act(ipre[I4, :], ps[7][0:4, :], AF.Identity, bias=bgi[0:4, 0:1])
        k.act(fl[I4, :], ps[6][0:4, :], AF.Identity, bias=bgf[0:4, 0:1])
        k.act(fl[I4, :], fl[I4, :], AF.Exp, scale=-1.0)
        k.ts("dve", fl[I4, :], fl[I4, :], 1.0, None, op0=ALU.add)
        k.act(fl[I4, :], fl[I4, :], AF.Ln)
        k.ts("dve", fl[I4, :], fl[I4, :], -1.0, None, op0=ALU.mult)
        k.scan("dve", Fr[I4, :], onesT[I4, :], fl[I4, :], Fc[0:4, 0:1], ALU.mult, ALU.add)
        k.tt("dve", gg[I4, :], ipre[I4, :], Fr[I4, :], ALU.subtract)
        k.scan("dve", Mr[I4, :], gg[I4, :], gg[I4, :], Mc[0:4, 0:1], ALU.max, ALU.max)
        k.copy("pool", Fc[0:4, 0:1], Fr[I4, T - 1:T])
        k.copy("pool", Mc[0:4, 0:1], Mr[I4, T - 1:T])
        k.tt("dve", emt[I4, :], Fr[I4, :], Mr[I4, :], ALU.add)
        k.act(emt[I4, :], emt[I4, :], AF.Exp, scale=-1.0)
        if _stage <= 2:
            return
        for hd in range(4):
            cs_v = cst_d.re(cst_d.ap[hd].rearrange("(dt p) v -> p dt v", p=128))
            if ci == 0:
                k.memset("pool", Cf, 0.0)
            else:
                k.dma("sp", Cf, cs_v)
            k.copy("act", Cb, Cf)
            k.mm(ps[0], sel4[0:4, hd * 128:(hd + 1) * 128], Mr[I4, :])
            Mbc = tA
            k.copy("act", Mbc, ps[0])
            for q in range(NQ):
                if _stage <= 3:
                    continue
                qs = slice(q * 128, (q + 1) * 128)
                k.transpose(ps[6][:, 136:140], gg[I4, qs], ident[0:4, 0:4])
                k.transpose(ps[6][:, 140:144], emt[I4, qs], ident[0:4, 0:4])
                k.copy("act", cols[:, 0:8], ps[6][:, 136:144])
                _sub = int(_os.environ.get("MLSTM_SUB", "9"))
                if _sub <= 1:
                    continue
                mu_st = mu_rep[:, hd:hd + 1]
                mu_new = Mbc[:, (q + 1) * 128 - 1:(q + 1) * 128]
                k.act(cols[:, 8:9], mu_new, AF.Exp, bias=cols[:, hd:hd + 1], scale=-1.0)
                k.act(cols[:, 9:10], mu_new, AF.Exp, bias=mu_st, scale=-1.0)
                if _sub <= 2:
                    continue
                k.act(wrow, Mbc[:, qs], AF.Exp, bias=mu_st, scale=-1.0)
                if _sub <= 3:
                    continue
                k.tt("dve", tmpD, maskneg, Mbc[:, qs], ALU.subtract)
                k.act(Dt_, tmpD, AF.Exp, bias=cols[:, hd:hd + 1])
                if _stage <= 4:
                    continue
                for dt in range(4):
                    j = hd * 4 + dt
                    ds = slice(dt * 128, (dt + 1) * 128)
                    k.mm(ps[2][:, ds], BD[0][:, j, :], xcbm[j][:, qs])
                    k.mm(ps[3][:, ds], BD[1][:, j, :], xcbm[j][:, qs])
                    k.mm(ps[4][:, ds], xcbm[j][:, qs], BD[1][:, j, :])
                    k.mm(ps[5][:, ds], xmb[j][:, qs], BD[2][:, j, :])
                k.copy("act", qTb, r3(ps[2], 4, 128))
                k.tt("dve", qwb, qTb, wrow.re(wrow.ap.unsqueeze(1).to_broadcast([128, 4, 128])), ALU.mult)
                k.act(kTb, r3(ps[3], 4, 128), AF.Identity, scale=KS)
                k.ts("dve", cols[:, 14:15], cols[:, 8:9], KS, None, op0=ALU.mult)
                k.act(ktw, ps[4], AF.Identity, scale=cols[:, 14:15])
                k.copy("act", vtm, ps[5])
                if _stage <= 5:
                    continue
                for dt in range(4):
                    k.mm(ps[6][:, 0:128], kTb[:, dt, :], qTb[:, dt, :], start=(dt == 0), stop=(dt == 3))
                k.tt("dve", smat, ps[6][:, 0:128], Dt_, ALU.mult)
                for dt in range(4):
                    k.mm(ps[1], qwb[:, dt, :], Cb[:, dt, :], start=(dt == 0), stop=False)
                k.mm(ps[1], smat, vtm, start=False, stop=True)
                for dt in range(4):
                    k.mm(ps[6][:, 128:129], qwb[:, dt, :], nb[:, hd * 4 + dt, 0:1], start=(dt == 0), stop=False)
                k.mm(ps[6][:, 128:129], smat, ones_bf[:, 0:1], start=False, stop=True)
                k.act(cols[:, 10:11], ps[6][:, 128:129], AF.Abs)
                k.tt("dve", cols[:, 10:11], cols[:, 10:11], cols[:, 4 + hd:5 + hd], ALU.max)
                k.recip(cols[:, 10:11], cols[:, 10:11])
                if _stage <= 6:
                    continue
                k.act(hs, ps[1], AF.Identity, scale=cols[:, 10:11], accum=cols[:, 11:12])
                k.ts("dve", cols[:, 11:12], cols[:, 11:12], -1.0 / 512, None, op0=ALU.mult)
                k.ts("dve", hs, hs, cols[:, 11:12], None, op0=ALU.add)
                k.act(tD, hs, AF.Square, accum=cols[:, 12:13])
                k.act(cols[:, 12:13], cols[:, 12:13], AF.Sqrt, bias=g.eps_col[:, 0:1], scale=1.0 / 512)
                k.recip(cols[:, 12:13], cols[:, 12:13])
                k.ts("dve", hs, hs, cols[:, 12:13], None, op0=ALU.mult)
                for dt in range(4):
                    k.transpose(ps[7][:, dt * 128:(dt + 1) * 128], hs[:, dt * 128:(dt + 1) * 128], ident)
                for dt in range(4):
                    j = hd * 4 + dt
                    k.ts("dve", fm[j][:, qs], ps[7][:, dt * 128:(dt + 1) * 128], mng[:, j:j + 1], None, op0=ALU.mult)
                if _stage <= 7:
                    continue
                for dt in range(4):
                    ds = slice(dt * 128, (dt + 1) * 128)
                    pc = ps[2 + dt]
                    k.mm(pc, ktw[:, ds], vtm)
                    k.mm(ps[6][:, 130 + dt:131 + dt], ktw[:, ds], ones_bf[:, 0:1])
                    k.stt("dve", Cf[:, dt, :], Cf[:, dt, :], cols[:, 9:10], pc, ALU.mult, ALU.add)
                    k.copy("act", Cb[:, dt, :], Cf[:, dt, :])
                k.stt("dve", nf[:, hd * 4:hd * 4 + 4], nf[:, hd * 4:hd * 4 + 4], cols[:, 9:10], ps[6][:, 130:134],
                      ALU.mult, ALU.add)
                k.copy("act", nb[:, hd * 4:hd * 4 + 4, 0], nf[:, hd * 4:hd * 4 + 4])
                k.copy("act", mu_rep[:, hd:hd + 1], mu_new)
            k.dma("sp", cs_v, Cf)
        k.spacer = None
        for j in range(16):
            if j % 2 == 0:
                k.dma("pool", wgu_ring[3], w_up_d.re(w_up_d.ap[0, :, 2048 + j * 128:2048 + (j + 2) * 128]
                                                     .rearrange("(kt p) n -> p kt n", p=128)))
            proj(ps[j % 2], wgu_ring[3], (j % 2) * 128)
            k.act(tA, ps[j % 2], AF.Silu)
            k.stt("dve", tB, xcbm[j], msk[:, j:j + 1], fm[j], ALU.mult, ALU.add)
            k.tt("dve", hid[j], tB, tA, ALU.mult)
        for m in range(8):
            wo = wd_ring[1]
            k.dma("pool", wo[:, 0:16, :], w_dn_d.re(w_dn_d.ap[0, :, m * 128:(m + 1) * 128]
                                                    .rearrange("(kt p) n -> p kt n", p=128)))
            pd = ps[4 + m % 2]
            for kt in range(16):
                k.mm(pd, wo[:, kt, :], hid[kt], start=(kt == 0), stop=(kt == 15))
            k.stt("dve", xT[m], pd, g.gm_col(1, 1, m), xT[m], ALU.mult, ALU.add)

    return [hybrid, mlstm]


_NC_CACHE = {}


def kernel(**inputs):
    x = np.asarray(inputs["x"])
    B, S, _ = x.shape
    T = 512
    key = (S, T)
    if key not in _NC_CACHE:
        _NC_CACHE[key] = build(S, T, n_sub=6, final=True)
    nc = _NC_CACHE[key]
    in_maps = [make_inputs(inputs, b, S) for b in range(B)]
    res = run_bass_kernel_spmd(nc, in_maps, core_ids=list(range(B)))
    out = np.stack([np.asarray(r["outT"]).T for r in res.results], axis=0)
    return np.ascontiguousarray(out.astype(np.float32))
```
